# Optimizing a Trainium2 kernel written in Bass

```python
import math
import jax, jax.numpy as jnp
from jax import lax
import numpy as np

D_MODEL = 2048
BATCH = 4
SEQ = 2048
DEPTH = 4
DEC_BATCH = 128
DEC_SEQ = 1
PAST_LEN = 16384
PAGE_SIZE = 128

N_MIXERS = 2
POOL_WINDOWS = (2, 4, 8, 16)
N_POOL_GROUPS = len(POOL_WINDOWS)
POOL_GROUP = D_MODEL // N_POOL_GROUPS
POOL_CTX = max(POOL_WINDOWS) - 1
GLA_HEADS = 4
GLA_KEY_DIM = D_MODEL // 2
GLA_VAL_DIM = D_MODEL
GLA_DK = GLA_KEY_DIM // GLA_HEADS
GLA_DV = GLA_VAL_DIM // GLA_HEADS
GLA_GATE_RANK = 16
GLA_GATE_TEMP = 16.0
GLA_CHUNK = 16
GLA_IN = 2 * GLA_KEY_DIM + 2 * GLA_VAL_DIM + GLA_GATE_RANK
D_FF = 4 * D_MODEL
LN_EPS = 1e-5
RMS_EPS = 1e-5
DN_ALPHA = (2 * DEPTH) ** 0.25
DN_BETA = (8 * DEPTH) ** -0.25
N_POOL_LAYERS = (DEPTH + 1) // 2
N_GLA_LAYERS = DEPTH // 2

kernel_name = "hybrid_pool_gla_deepnorm_step"


def layer_norm(x, g, b):
    xf = x.astype(jnp.float32)
    mu = jnp.mean(xf, axis=-1, keepdims=True)
    var = jnp.mean(jnp.square(xf - mu), axis=-1, keepdims=True)
    y = (xf - mu) * lax.rsqrt(var + LN_EPS) * g.astype(jnp.float32) + b.astype(jnp.float32)
    return y.astype(x.dtype)


def pool_mixer(x_ctx, x, pos0, w_groups, scale):
    n_ctx = x_ctx.shape[1]
    L = x.shape[1]
    xe = jnp.concatenate([x_ctx, x.astype(x_ctx.dtype)], axis=1)
    cs = jnp.cumsum(xe.astype(jnp.float32), axis=1)
    pos = pos0 + n_ctx + jnp.arange(L, dtype=jnp.int32)
    xf = x.astype(jnp.float32)
    outs = []
    for g, w in enumerate(POOL_WINDOWS):
        sl = slice(g * POOL_GROUP, (g + 1) * POOL_GROUP)
        csp = jnp.pad(cs[..., sl], ((0, 0), (w, 0), (0, 0)))
        win = csp[:, n_ctx + w:] - csp[:, n_ctx:n_ctx + L]
        count = jnp.minimum(pos + 1, w).astype(jnp.float32)[None, :, None]
        p = win / count - xf[..., sl]
        outs.append(jnp.einsum('blc,cd->bld', p, w_groups[g].astype(jnp.float32)))
    y = jnp.concatenate(outs, axis=-1) * scale.astype(jnp.float32)
    new_state = xe[:, -POOL_CTX:]
    return y.astype(x.dtype), new_state


def gla_mixer(x, s0, w_in, w_gate_up, gate_bias, norm_w, w_out):
    B, L, _ = x.shape
    K, V = GLA_KEY_DIM, GLA_VAL_DIM
    proj = jnp.einsum('bld,de->ble', x, w_in)
    q, k, v, og, gk_low = jnp.split(proj, [K, 2 * K, 2 * K + V, 2 * K + 2 * V], axis=-1)
    gk = jnp.einsum('blr,rk->blk', gk_low, w_gate_up) + gate_bias
    log_a = jax.nn.log_sigmoid(gk.astype(jnp.float32)) / GLA_GATE_TEMP
    q = q.astype(jnp.float32).reshape(B, L, GLA_HEADS, GLA_DK) * (GLA_DK ** -0.5)
    k = k.astype(jnp.float32).reshape(B, L, GLA_HEADS, GLA_DK)
    v = v.astype(jnp.float32).reshape(B, L, GLA_HEADS, GLA_DV)
    log_a = log_a.reshape(B, L, GLA_HEADS, GLA_DK)
    C = min(GLA_CHUNK, L)
    n_chunks = -(-L // C)
    pad = n_chunks * C - L

    def to_chunks(t):
        t = jnp.pad(t, ((0, 0), (0, pad), (0, 0), (0, 0)))
        t = t.reshape(B, n_chunks, C, GLA_HEADS, t.shape[-1])
        return jnp.transpose(t, (1, 0, 3, 2, 4))

    mask = jnp.tril(jnp.ones((C, C), dtype=bool))[:, :, None]

    def step(S, inp):
        qc, kc, vc, lac = inp
        bcum = jnp.cumsum(lac, axis=2)
        diff = bcum[:, :, :, None, :] - bcum[:, :, None, :, :]
        decay = jnp.exp(jnp.where(mask, diff, -jnp.inf))
        attn = jnp.einsum('bhik,bhjk,bhijk->bhij', qc, kc, decay)
        o = jnp.einsum('bhij,bhjv->bhiv', attn, vc) + \
            jnp.einsum('bhik,bhkv->bhiv', qc * jnp.exp(bcum), S)
        b_last = bcum[:, :, -1:, :]
        S_new = S * jnp.exp(b_last[:, :, 0, :, None]) + \
            jnp.einsum('bhjk,bhjv->bhkv', kc * jnp.exp(b_last - bcum), vc)
        return S_new, o

    S_final, o = lax.scan(step, s0.astype(jnp.float32),
                          (to_chunks(q), to_chunks(k), to_chunks(v), to_chunks(log_a)))
    o = jnp.transpose(o, (1, 0, 3, 2, 4)).reshape(B, n_chunks * C, GLA_HEADS, GLA_DV)[:, :L]
    o = o * lax.rsqrt(jnp.mean(jnp.square(o), axis=-1, keepdims=True) + RMS_EPS) * norm_w.astype(jnp.float32)
    o = o.reshape(B, L, V) * jax.nn.silu(og.astype(jnp.float32))
    y = jnp.einsum('blv,vd->bld', o.astype(x.dtype), w_out)
    return y, S_final.astype(x.dtype)


def sq_relu_mlp(x, w1, b1, w2, b2):
    h = jnp.square(jax.nn.relu(jnp.einsum('bld,df->blf', x, w1) + b1))
    return jnp.einsum('blf,fd->bld', h, w2) + b2


def trunk(x, pool_ctx, gla_init, pos0, pool_w, pool_scale, gla_w_in, gla_w_gate_up, gla_gate_bias,
          gla_norm_w, gla_w_out, ln_mix_g, ln_mix_b, mlp_w1, mlp_b1, mlp_w2, mlp_b2, ln_ffn_g, ln_ffn_b):
    pool_states, gla_states = [], []
    for i in range(DEPTH):
        j = i // N_MIXERS
        if i % N_MIXERS == 0:
            ctx = x[:, :0] if pool_ctx is None else pool_ctx[j]
            h, st = pool_mixer(ctx, x, pos0, pool_w[j], pool_scale[j])
            pool_states.append(st)
        else:
            s0 = (jnp.zeros((x.shape[0], GLA_HEADS, GLA_DK, GLA_DV), jnp.float32)
                  if gla_init is None else gla_init[j])
            h, st = gla_mixer(x, s0, gla_w_in[j], gla_w_gate_up[j], gla_gate_bias[j],
                              gla_norm_w[j], gla_w_out[j])
            gla_states.append(st)
        x = layer_norm(DN_ALPHA * x + h, ln_mix_g[i], ln_mix_b[i])
        x = layer_norm(DN_ALPHA * x + sq_relu_mlp(x, mlp_w1[i], mlp_b1[i], mlp_w2[i], mlp_b2[i]),
                       ln_ffn_g[i], ln_ffn_b[i])
    return x, jnp.stack(pool_states), jnp.stack(gla_states)


def setup_inputs(seed: int = 0) -> dict:
    key = jax.random.key(seed)
    ks = jax.random.split(key, 24)
    f32 = jnp.float32
    nrm = lambda k, shape, s: jax.random.normal(k, shape, f32) * s
    return {
        "x_prompt": nrm(ks[0], (BATCH, SEQ, D_MODEL), 1.0),
        "x_sample": nrm(ks[1], (DEC_BATCH, DEC_SEQ, D_MODEL), 1.0),
        "state_pool": nrm(ks[2], (N_POOL_LAYERS, DEC_BATCH, POOL_CTX, D_MODEL), 1.0),
        "state_gla": nrm(ks[3], (N_GLA_LAYERS, DEC_BATCH, GLA_HEADS, GLA_DK, GLA_DV), 0.5),
        "pool_w": nrm(ks[4], (N_POOL_LAYERS, N_POOL_GROUPS, POOL_GROUP, POOL_GROUP), POOL_GROUP ** -0.5 * DN_BETA),
        "pool_scale": 1.0 + nrm(ks[5], (N_POOL_LAYERS, D_MODEL), 0.1),
        "gla_w_in": nrm(ks[6], (N_GLA_LAYERS, D_MODEL, GLA_IN), D_MODEL ** -0.5),
        "gla_w_gate_up": nrm(ks[7], (N_GLA_LAYERS, GLA_GATE_RANK, GLA_KEY_DIM), GLA_GATE_RANK ** -0.5),
        "gla_gate_bias": nrm(ks[8], (N_GLA_LAYERS, GLA_KEY_DIM), 0.1),
        "gla_norm_w": 1.0 + nrm(ks[9], (N_GLA_LAYERS, GLA_DV), 0.1),
        "gla_w_out": nrm(ks[10], (N_GLA_LAYERS, GLA_VAL_DIM, D_MODEL), GLA_VAL_DIM ** -0.5 * DN_BETA),
        "ln_mix_g": 1.0 + nrm(ks[11], (DEPTH, D_MODEL), 0.1),
        "ln_mix_b": nrm(ks[12], (DEPTH, D_MODEL), 0.02),
        "mlp_w1": nrm(ks[13], (DEPTH, D_MODEL, D_FF), D_MODEL ** -0.5),
        "mlp_b1": nrm(ks[14], (DEPTH, D_FF), 0.02),
        "mlp_w2": nrm(ks[15], (DEPTH, D_FF, D_MODEL), D_FF ** -0.5 * DN_BETA),
        "mlp_b2": nrm(ks[16], (DEPTH, D_MODEL), 0.02),
        "ln_ffn_g": 1.0 + nrm(ks[17], (DEPTH, D_MODEL), 0.1),
        "ln_ffn_b": nrm(ks[18], (DEPTH, D_MODEL), 0.02),
    }


def reference(x_prompt, x_sample, state_pool, state_gla, pool_w, pool_scale, gla_w_in, gla_w_gate_up,
              gla_gate_bias, gla_norm_w, gla_w_out, ln_mix_g, ln_mix_b, mlp_w1, mlp_b1, mlp_w2, mlp_b2,
              ln_ffn_g, ln_ffn_b):
    y_prompt, new_pool_prompt, new_gla_prompt = trunk(
        x_prompt, None, None, 0, pool_w, pool_scale, gla_w_in, gla_w_gate_up, gla_gate_bias,
        gla_norm_w, gla_w_out, ln_mix_g, ln_mix_b, mlp_w1, mlp_b1, mlp_w2, mlp_b2, ln_ffn_g, ln_ffn_b)
    y_sample, new_pool_sample, new_gla_sample = trunk(
        x_sample, state_pool, state_gla, PAST_LEN - POOL_CTX, pool_w, pool_scale, gla_w_in,
        gla_w_gate_up, gla_gate_bias, gla_norm_w, gla_w_out, ln_mix_g, ln_mix_b, mlp_w1, mlp_b1,
        mlp_w2, mlp_b2, ln_ffn_g, ln_ffn_b)
    return (y_prompt, y_sample, new_pool_prompt, new_gla_prompt, new_pool_sample, new_gla_sample)
```

```python
import math
from contextlib import ExitStack

import numpy as np
import concourse.bass as bass
import concourse.mybir as mybir
from concourse.bass_utils import run_bass_kernel_spmd

F32 = mybir.dt.float32
BF16 = mybir.dt.bfloat16
AF = mybir.ActivationFunctionType
ALU = mybir.AluOpType
AX = mybir.AxisListType

D = 2048
DC = 16
DFF = 8192
FC = 64
DEPTH = 4
NS = 16
HALO = 16
ALPHA = (2 * DEPTH) ** 0.25
INV_ALPHA = 1.0 / ALPHA
LN_EPS = 1e-5
EPS_P = LN_EPS / (ALPHA * ALPHA)
GLA_IN = 6160
G_MLP = 4

ROW_B1 = 0
ROW_B2 = ROW_B1 + 4 * 64
ROW_GM = ROW_B2 + 4 * 16
ROW_BM = ROW_GM + 4 * 16
ROW_GF = ROW_BM + 4 * 16
ROW_BF = ROW_GF + 4 * 16
ROW_PS = ROW_BF + 4 * 16
NROWS = ROW_PS + 2 * 16
NROWS_PAD = 640


class KB:
    ENG = ("pe", "act", "dve", "pool", "sp")

    def __init__(self, nc):
        self.nc = nc
        self.ops = {e: [] for e in self.ENG}
        self.cnt = {}
        self.seen = {e: {} for e in self.ENG}
        self.trk = {}

    def op(self, eng, fn, reads=(), writes=(), inc=True, dma=None):
        deps = {}

        def add(d):
            for k, v in d.items():
                if deps.get(k, 0) < v:
                    deps[k] = v

        for key in reads:
            t = self.trk.get(key)
            if t:
                add(t["w"])
        for key in writes:
            t = self.trk.get(key)
            if t:
                add(t["w"])
                add(t["r"])
        waits = []
        for k, v in deps.items():
            if k == "pe" and eng == "pe":
                continue
            if self.seen[eng].get(k, 0) < v:
                self.seen[eng][k] = v
                waits.append((k, v))
        if dma is not None:
            prev = self.cnt.get(dma, 0)
            if prev and self.seen[eng].get(dma, 0) < prev:
                self.seen[eng][dma] = prev
                waits.append((dma, prev))
            self.cnt[dma] = prev + 16
            my = {dma: prev + 16}
            incr = (dma, 16)
        else:
            c = self.cnt.get(eng, 0)
            if inc:
                self.cnt[eng] = c + 1
                incr = (eng, 1)
            else:
                incr = None
            my = {eng: c + 1}
        for key in reads:
            t = self.trk.setdefault(key, {"w": {}, "r": {}})
            for k, v in my.items():
                if t["r"].get(k, 0) < v:
                    t["r"][k] = v
        for key in writes:
            self.trk[key] = {"w": dict(my), "r": {}}
        self.ops[eng].append((waits, fn, incr))

    def barrier(self, engs=("pe", "act", "dve")):
        for e in engs:
            waits = []
            for k in engs:
                v = self.cnt.get(k, 0)
                if k != e and v and self.seen[e].get(k, 0) < v:
                    self.seen[e][k] = v
                    waits.append((k, v))
            if waits:
                self.ops[e].append((waits, None, None))

    def emit(self, final_wait_eng="sp"):
        nc = self.nc
        keys = list(self.cnt.keys())
        with ExitStack() as es:
            sems = {}
            for i, k in enumerate(keys):
                sems[k] = es.enter_context(nc.semaphore("s%d" % i))
            fw = [(k, v) for k, v in self.cnt.items() if k not in self.ENG]
            fw += [(k, self.cnt[k]) for k in ("pe", "act", "dve") if k in self.cnt]
            self.ops[final_wait_eng].append((fw, None, None))

            def replay(name, e):
                for waits, fn, incr in self.ops[name]:
                    for k, v in waits:
                        e.wait_ge(sems[k], v)
                    if fn is None:
                        continue
                    ins = fn(e)
                    if incr is not None:
                        ins.then_inc(sems[incr[0]], incr[1])

            with nc.Block() as block:
                @block.tensor
                def _(e):
                    replay("pe", e)

                @block.scalar
                def _(e):
                    replay("act", e)

                @block.vector
                def _(e):
                    replay("dve", e)

                @block.gpsimd
                def _(e):
                    replay("pool", e)

                @block.sync
                def _(e):
                    replay("sp", e)


class Arena:
    def __init__(self, h, nwords):
        self.h = h
        self.hb = h.bitcast(BF16)
        self.n = nwords
        self.off = 0

    def reset(self):
        self.off = 0

    def _take(self, words):
        o = self.off
        self.off += words
        assert self.off <= self.n, ("arena overflow", self.off, self.n)
        return o

    @staticmethod
    def _shape(ap, shape):
        if len(shape) == 2:
            return ap
        if len(shape) == 3:
            return ap.rearrange("p (a b) -> p a b", a=shape[1])
        if len(shape) == 4:
            return ap.rearrange("p (a b c) -> p a b c", a=shape[1], b=shape[2])
        raise ValueError(shape)

    def f32(self, shape):
        n = int(np.prod(shape[1:]))
        o = self._take(n)
        return self._shape(self.h[:, o:o + n], shape)

    def bf16(self, shape):
        n = int(np.prod(shape[1:]))
        o = self._take((n + 1) // 2)
        return self._shape(self.hb[:, 2 * o:2 * o + n], shape)


CW = 576


def build(cfg):
    NPASS = cfg["npass"]
    TP = cfg["tp"]
    NL = cfg.get("layers", DEPTH)
    MIX = cfg.get("mixers", "all")
    NTOK = NPASS * TP
    COLS = HALO + TP + NS
    C0 = HALO
    SC0 = C0 + TP
    NTT = TP // 128

    nc = bass.Bass("TRN2", target_bir_lowering=False)
    kb = KB(nc)

    def din(name, shape):
        return nc.dram_tensor(name, list(shape), F32, kind="ExternalInput")

    def dout(name, shape):
        return nc.dram_tensor(name, list(shape), F32, kind="ExternalOutput")

    xp = din("xp", [NTOK, D])
    xs = din("xs", [NS, D])
    vecs = din("vecs", [NROWS_PAD, 128])
    consts = din("consts", [128, CW])
    w1 = din("mlp_w1", [DEPTH, D, DFF])
    w2 = din("mlp_w2", [DEPTH, DFF, D])
    yp = dout("yp", [NTOK, D])
    ys = dout("ys", [NS, D])
    if MIX != "none":
        spool = din("spool", [2, NS, 15, D])
        pool_w = din("pool_w", [2, 4, 512, 512])
        npp = dout("npp", [2, 15, D])
        nps = dout("nps", [2, NS, 15, D])
    if MIX == "all":
        sgla = din("sgla", [2, NS, 4, 256, 512])
        w_in = din("gla_w_in", [2, D, GLA_IN])
        d_wgu = din("gla_wgu", [2, 16, 1024])
        d_gbias = din("gla_gbias", [2, 1024])
        d_normw = din("gla_normw", [2, 512])
        w_out = din("gla_w_out", [2, D, D])
        ngp = dout("ngp", [2, 4, 256, 512])
        ngs = dout("ngs", [2, NS, 4, 256, 512])

    sb = lambda name, shape, dt=F32: nc.alloc_sbuf_tensor(name, list(shape), dt)
    xT = sb("xT", [128, DC, COLS])
    xTb = sb("xTb", [128, DC, COLS], BF16)
    cols = sb("cols", [128, NROWS_PAD])
    cst = sb("cst", [128, CW])
    identb = sb("identb", [128, 128], BF16)
    onesb = sb("onesb", [128, 128], BF16)
    epsc = sb("epsc", [128, 2])
    fillb = sb("fillb", [128, 512], BF16)
    stage = [sb("stage%d" % i, [128, D]) for i in range(2)]
    WA = [sb("WA%d" % i, [128, 4096], BF16) for i in range(3)]
    WB = [sb("WB%d" % i, [128, G_MLP, D], BF16) for i in range(2)]
    halo_save = [sb("halo%d" % i, [128, DC, 16]) for i in range(2)]
    if MIX == "all":
        wgu = sb("wgu", [16, 1024])
        gbias_bc = sb("gbias_bc", [128, 1024])
        normw_bc = sb("normw_bc", [128, 512])
        Ss = [sb("Ss%d" % i, [128, 2, 512]) for i in range(2)]
        Sn = [sb("Sn%d" % i, [128, 2, 512]) for i in range(2)]
        Sst = sb("Sst", [128, 2, 512])
    rem = int(nc.sbuf_bytes_remaining) - 256
    AW = rem // 4
    ar = Arena(sb("arena", [128, AW]), AW)

    ps = [nc.alloc_psum_tensor("ps%d" % i, [128, 512], F32) for i in range(8)]
    psb = [t.bitcast(BF16) for t in ps]
    ident = cst[:, 0:128]
    Umask = cst[:, 128:256]
    invc = cst[:, 256:320]
    eyeb = cst[:, 320:576].rearrange("p (a b) -> p a b", a=16)

    misc_i = [0]

    def misc_sem():
        misc_i[0] += 1
        return "m%d" % (misc_i[0] % 6)

    def wa_view(s, c, n):
        return WA[s][:, 0:c * n].rearrange("p (c n) -> p c n", c=c)

    kb.op("sp", lambda e: e.dma_start(out=cst[:], in_=consts.ap()), writes=["cst"], dma=misc_sem())
    kb.op("dve", lambda e: e.tensor_copy(out=identb[:], in_=cst[:, 0:128]), reads=["cst"], writes=["identb"])
    kb.op("dve", lambda e: e.memset(onesb[:], 1.0 / D), writes=["onesb"])
    kb.op("dve", lambda e: e.memset(fillb[:], 0.0), writes=["fillb"])
    kb.op("dve", lambda e: e.memset(epsc[:, 0:1], EPS_P), writes=["epsc"])
    kb.op("dve", lambda e: e.memset(epsc[:, 1:2], 1e-5), writes=["epsc"])
    for t in range(NROWS_PAD // 128):
        st = stage[t % 2]
        kb.op("sp", lambda e, st=st, t=t: e.dma_start(out=st[:, 0:128], in_=vecs.ap()[t * 128:(t + 1) * 128, :]),
              writes=[("stage", t % 2)], dma=misc_sem())
        kb.op("pe", lambda e, st=st: e.transpose(out=ps[0][:, 0:128], in_=st[:, 0:128], identity=ident),
              reads=[("stage", t % 2), "cst"], writes=[("ps", 0)])
        kb.op("act", lambda e, t=t: e.copy(out=cols[:, t * 128:(t + 1) * 128], in_=ps[0][:, 0:128]),
              reads=[("ps", 0)], writes=["cols"])
    kb.op("dve", lambda e: e.tensor_scalar(out=cols[:, ROW_B2:ROW_B2 + 64], in0=cols[:, ROW_B2:ROW_B2 + 64],
                                           scalar1=INV_ALPHA, scalar2=None, op0=ALU.mult),
          reads=["cols"], writes=["cols"])
    kb.op("dve", lambda e: e.tensor_scalar(out=cols[:, ROW_PS:ROW_PS + 32], in0=cols[:, ROW_PS:ROW_PS + 32],
                                           scalar1=INV_ALPHA, scalar2=None, op0=ALU.mult),
          reads=["cols"], writes=["cols"])

    wa_i = [0]
    wb_i = [0]

    def load_WA(src_ap, c, n):
        s = wa_i[0] % 3
        wa_i[0] += 1
        dst = wa_view(s, c, n)
        kb.op("pool", lambda e: e.dma_start(out=dst, in_=src_ap), writes=[("WA", s)], dma="wa%d" % s)
        return s

    def load_WB(src_ap):
        s = wb_i[0] % 2
        wb_i[0] += 1
        kb.op("pool", lambda e: e.dma_start(out=WB[s][:], in_=src_ap), writes=[("WB", s)], dma="wb%d" % s)
        return s

    def xkeys():
        return [("xT", i) for i in range(DC)]

    def xbkeys():
        return [("xTb", i) for i in range(DC)]

    def groups(p):
        g = [(C0 + i * 512, min(512, TP - i * 512)) for i in range((TP + 511) // 512)]
        if p == 0:
            g.append((SC0, NS))
        return g

    def emit_in(p):
        for i in range(NTT):
            st = stage[i % 2]
            r0 = p * TP + i * 128
            kb.op("sp", lambda e, st=st, r0=r0: e.dma_start(out=st[:], in_=xp.ap()[r0:r0 + 128, :]),
                  writes=[("stage", i % 2)], dma=misc_sem())
            for q in range(4):
                b = q % 2
                for cc in range(4):
                    c = q * 4 + cc
                    kb.op("pe", lambda e, st=st, c=c, cc=cc, b=b: e.transpose(
                        out=ps[b][:, cc * 128:(cc + 1) * 128], in_=st[:, c * 128:(c + 1) * 128], identity=ident),
                        reads=[("stage", i % 2), "cst"], writes=[("ps", b)], inc=(cc == 3))
                col = C0 + i * 128
                kb.op("act", lambda e, q=q, b=b, col=col: e.copy(
                    out=xT[:, q * 4:(q + 1) * 4, col:col + 128],
                    in_=ps[b][:].rearrange("p (a t) -> p a t", a=4)),
                    reads=[("ps", b)], writes=[("xT", q * 4 + k) for k in range(4)])
                kb.op("dve", lambda e, q=q, col=col: e.tensor_copy(
                    out=xTb[:, q * 4:(q + 1) * 4, col:col + 128],
                    in_=xT[:, q * 4:(q + 1) * 4, col:col + 128]),
                    reads=[("xT", q * 4 + k) for k in range(4)], writes=[("xTb", q * 4 + k) for k in range(4)])
        if p == 0:
            st = stage[0]
            kb.op("sp", lambda e: e.dma_start(out=st[0:NS, :], in_=xs.ap()), writes=[("stage", 0)], dma=misc_sem())
            for c in range(DC):
                kb.op("pe", lambda e, c=c: e.transpose(out=ps[2][:, c * NS:(c + 1) * NS],
                                                        in_=st[0:NS, c * 128:(c + 1) * 128],
                                                        identity=cst[0:NS, 0:NS]),
                      reads=[("stage", 0), "cst"], writes=[("ps", 2)], inc=(c == DC - 1))
            kb.op("act", lambda e: e.copy(out=xT[:, :, SC0:SC0 + NS],
                                          in_=ps[2][:, 0:DC * NS].rearrange("p (c s) -> p c s", c=DC)),
                  reads=[("ps", 2)], writes=xkeys())
            kb.op("dve", lambda e: e.tensor_copy(out=xTb[:, :, SC0:SC0 + NS], in_=xT[:, :, SC0:SC0 + NS]),
                  reads=xkeys(), writes=xbkeys())

    def emit_rows_out(src_of_chunk, M, dst_ap, sidx):
        st = stage[sidx]
        for q in range(4):
            for cc in range(4):
                c = q * 4 + cc
                kb.op("pe", lambda e, c=c, cc=cc: e.transpose(
                    out=ps[3][0:M, cc * 128:(cc + 1) * 128], in_=src_of_chunk(c), identity=ident),
                    reads=xkeys() + ["cst", "halo"], writes=[("ps", 3)], inc=(cc == 3))
            kb.op("act", lambda e, q=q: e.copy(out=st[0:M, q * 512:(q + 1) * 512], in_=ps[3][0:M, :]),
                  reads=[("ps", 3)], writes=[("stage", sidx)])
        kb.op("sp", lambda e: e.dma_start(out=dst_ap, in_=st[0:M, :]), reads=[("stage", sidx)], dma=misc_sem())

    def emit_out(p):
        for i in range(NTT):
            col = C0 + i * 128
            r0 = p * TP + i * 128
            emit_rows_out(lambda c, col=col: xT[:, c, col:col + 128], 128, yp.ap()[r0:r0 + 128, :], i % 2)
        if p == 0:
            emit_rows_out(lambda c: xT[:, c, SC0:SC0 + NS], NS, ys.ap(), 0)

    MLP_WORDS = 2 * ((G_MLP * COLS + 1) // 2) + 2 * (512 + NS)

    def emit_fill(n):
        for i in range(n):
            kb.op("pe", lambda e: e.matmul(ps[7][:, :], lhsT=fillb[:, 0:128], rhs=fillb[:, :], start=True, stop=True),
                  reads=["fillb"], writes=[("ps", 7)], inc=(i == n - 1))

    def emit_ln(p, grow, brow, barrier=True):
        if barrier:
            kb.barrier()
        ar.off = MLP_WORDS
        ln_mean = ar.f32([128, COLS])
        ln_rstd = ar.f32([128, COLS])
        ln_nmr = ar.f32([128, COLS])
        ln_zb = [ar.bf16([128, 4, COLS]) for _ in range(2)]
        ln_zq = [ar.bf16([128, 4, COLS]) for _ in range(2)]
        ln_u1 = ar.f32([128, 4, COLS])
        ln_u = [ln_u1, ln_u1]
        gs = groups(p)
        lo = C0
        hi = gs[-1][0] + gs[-1][1]
        n = hi - lo
        for q in range(4):
            b = q % 2
            xk = [("xT", q * 4 + k) for k in range(4)]
            kb.op("dve", lambda e, q=q, b=b: e.tensor_copy(out=ln_zb[b][:, :, lo:hi], in_=xT[:, q * 4:q * 4 + 4, lo:hi]),
                  reads=xk, writes=[("zb", b)])
            kb.op("act", lambda e, q=q, b=b: e.activation(out=ln_zq[b][:, :, lo:hi], in_=xT[:, q * 4:q * 4 + 4, lo:hi], func=AF.Square),
                  reads=xk, writes=[("zq", b)])
            for cc in range(4):
                c = q * 4 + cc
                for gi, (g0, gn) in enumerate(gs):
                    kb.op("pe", lambda e, b=b, cc=cc, g0=g0, gn=gn, gi=gi, c=c: e.matmul(
                        ps[gi][:, 0:gn], lhsT=onesb[:], rhs=ln_zb[b][:, cc, g0:g0 + gn], start=(c == 0), stop=(c == DC - 1)),
                        reads=[("zb", b), "onesb"], writes=[("ps", gi)], inc=False)
                    kb.op("pe", lambda e, b=b, cc=cc, g0=g0, gn=gn, gi=gi, c=c: e.matmul(
                        ps[4 + gi][:, 0:gn], lhsT=onesb[:], rhs=ln_zq[b][:, cc, g0:g0 + gn], start=(c == 0), stop=(c == DC - 1)),
                        reads=[("zq", b), "onesb"], writes=[("ps", 4 + gi)], inc=(cc == 3 and gi == len(gs) - 1))
        for gi, (g0, gn) in enumerate(gs):
            sl = slice(g0, g0 + gn)
            kb.op("act", lambda e, gi=gi, gn=gn, sl=sl: e.copy(out=ln_mean[:, sl], in_=ps[gi][:, 0:gn]),
                  reads=[("ps", gi)], writes=["ln_mean"])
            kb.op("dve", lambda e, sl=sl: e.tensor_tensor(out=ln_nmr[:, sl], in0=ln_mean[:, sl], in1=ln_mean[:, sl], op=ALU.mult),
                  reads=["ln_mean"], writes=["ln_nmr"])
            kb.op("dve", lambda e, gi=gi, gn=gn, sl=sl: e.tensor_tensor(out=ln_rstd[:, sl], in0=ps[4 + gi][:, 0:gn], in1=ln_nmr[:, sl], op=ALU.subtract),
                  reads=[("ps", 4 + gi), "ln_nmr"], writes=["ln_rstd"])
            kb.op("act", lambda e, sl=sl: e.activation(out=ln_rstd[:, sl], in_=ln_rstd[:, sl], func=AF.Sqrt, bias=epsc[:, 0:1], scale=1.0),
                  reads=["ln_rstd", "epsc"], writes=["ln_rstd"])
            kb.op("dve", lambda e, sl=sl: e.reciprocal(out=ln_rstd[:, sl], in_=ln_rstd[:, sl]),
                  reads=["ln_rstd"], writes=["ln_rstd"])
            kb.op("dve", lambda e, sl=sl: e.scalar_tensor_tensor(out=ln_nmr[:, sl], in0=ln_mean[:, sl], scalar=-1.0, in1=ln_rstd[:, sl], op0=ALU.mult, op1=ALU.mult),
                  reads=["ln_mean", "ln_rstd"], writes=["ln_nmr"])
        emit_fill(cfg.get("ln_fill", 0))
        rb = ln_rstd[:, lo:hi].unsqueeze(1).to_broadcast([128, 4, n])
        nb = ln_nmr[:, lo:hi].unsqueeze(1).to_broadcast([128, 4, n])
        for q in range(4):
            b = q % 2
            xk = [("xT", q * 4 + k) for k in range(4)]
            kb.op("dve", lambda e, q=q, b=b: e.tensor_tensor(out=ln_u[b][:, :, lo:hi], in0=xT[:, q * 4:q * 4 + 4, lo:hi], in1=rb, op=ALU.mult),
                  reads=xk + ["ln_rstd"], writes=["lnu"])
            kb.op("dve", lambda e, b=b: e.tensor_tensor(out=ln_u[b][:, :, lo:hi], in0=ln_u[b][:, :, lo:hi], in1=nb, op=ALU.add),
                  reads=["lnu", "ln_nmr"], writes=["lnu"])
            for cc in range(4):
                c = q * 4 + cc
                kb.op("act", lambda e, c=c, cc=cc, b=b: e.activation(out=xT[:, c, lo:hi], in_=ln_u[b][:, cc, lo:hi], func=AF.Identity,
                                                              bias=cols[:, brow + c:brow + c + 1], scale=cols[:, grow + c:grow + c + 1]),
                      reads=["lnu", "cols"], writes=[("xT", c)])
                kb.op("act", lambda e, c=c, cc=cc, b=b: e.activation(out=xTb[:, c, lo:hi], in_=ln_u[b][:, cc, lo:hi], func=AF.Identity,
                                                              bias=cols[:, brow + c:brow + c + 1], scale=cols[:, grow + c:grow + c + 1]),
                      reads=["lnu", "cols"], writes=[("xTb", c)])

    ybank = [0]

    def emit_accum_mm(gs, sB, src, skey, nk):
        for m in range(DC):
            for gi, (g0, gn) in enumerate(gs):
                bank = (4 + 2 * gi + (ybank[0] % 2)) if len(gs) > 1 else (4 + ybank[0] % 4)
                for k in range(nk):
                    kb.op("pe", lambda e, k=k, m=m, g0=g0, gn=gn, bank=bank: e.matmul(
                        ps[bank][:, 0:gn], lhsT=WB[sB][:, k, m * 128:(m + 1) * 128],
                        rhs=src[:, k, g0:g0 + gn], start=(k == 0), stop=(k == nk - 1)),
                        reads=[("WB", sB), (skey, k)], writes=[("ps", bank)], inc=(k == nk - 1))
                kb.op("dve", lambda e, m=m, g0=g0, gn=gn, bank=bank: e.scalar_tensor_tensor(
                    out=xT[:, m, g0:g0 + gn], in0=ps[bank][:, 0:gn], scalar=INV_ALPHA,
                    in1=xT[:, m, g0:g0 + gn], op0=ALU.mult, op1=ALU.add),
                    reads=[("ps", bank), ("xT", m)], writes=[("xT", m)])
            ybank[0] += 1

    def emit_mlp(p, l):
        ar.reset()
        hTs = [ar.bf16([128, G_MLP, COLS]) for _ in range(2)]
        rT = [ar.f32([128, 512 + NS]) for _ in range(2)]
        gs = groups(p)
        lo = C0
        hi = gs[-1][0] + gs[-1][1]
        for c in range(DC):
            kb.op("dve", lambda e, c=c: e.tensor_scalar(out=xT[:, c, lo:hi], in0=xT[:, c, lo:hi],
                                                        scalar1=cols[:, ROW_B2 + l * 16 + c:ROW_B2 + l * 16 + c + 1],
                                                        scalar2=None, op0=ALU.add),
                  reads=[("xT", c), "cols"], writes=[("xT", c)])
        st = {}

        def mm1(g):
            hT = hTs[g % 2]
            for jj in range(G_MLP):
                j = g * G_MLP + jj
                if j % 2 == 0:
                    f0 = j * 128
                    st["sA"] = load_WA(w1.ap()[l, :, f0:f0 + 256].rearrange("(c p) f -> p c f", p=128), DC, 256)
                sA = st["sA"]
                Wv = wa_view(sA, DC, 256)
                for gi, (g0, gn) in enumerate(gs):
                    bank = ((2 * gi) + (j % 2)) if len(gs) > 1 else (j % 4)
                    for dc in range(DC):
                        kb.op("pe", lambda e, Wv=Wv, j=j, dc=dc, g0=g0, gn=gn, bank=bank: e.matmul(
                            ps[bank][:, 0:gn], lhsT=Wv[:, dc, (j % 2) * 128:(j % 2) * 128 + 128],
                            rhs=xTb[:, dc, g0:g0 + gn], start=(dc == 0), stop=(dc == DC - 1)),
                            reads=[("WA", sA), ("xTb", dc)], writes=[("ps", bank)], inc=(dc == DC - 1))
                    rb = rT[j % 2]
                    roff = 0 if gi == 0 else 512
                    bcol = ROW_B1 + l * 64 + j
                    kb.op("act", lambda e, bank=bank, gn=gn, rb=rb, roff=roff, bcol=bcol: e.activation(
                        out=rb[:, roff:roff + gn], in_=ps[bank][:, 0:gn], func=AF.Relu,
                        bias=cols[:, bcol:bcol + 1], scale=1.0),
                        reads=[("ps", bank), "cols"], writes=[("rT", j % 2, gi)])
                    kb.op("act", lambda e, rb=rb, roff=roff, gn=gn, jj=jj, g0=g0, hT=hT: e.activation(
                        out=hT[:, jj, g0:g0 + gn], in_=rb[:, roff:roff + gn], func=AF.Square),
                        reads=[("rT", j % 2, gi)], writes=[(("hT", g % 2), jj)])

        def mm2(g):
            r0 = g * G_MLP * 128
            sB = load_WB(w2.ap()[l, r0:r0 + G_MLP * 128, :].rearrange("(j p) d -> p j d", p=128))
            emit_accum_mm(gs, sB, hTs[g % 2], ("hT", g % 2), G_MLP)

        NG = FC // G_MLP
        for g in range(NG):
            mm1(g)
            if g > 0:
                mm2(g - 1)
        mm2(NG - 1)

    def emit_pool(p, l):
        j = l // 2
        kb.barrier()
        ar.reset()
        hi = C0 + TP
        scr = [ar.f32([128, hi]) for _ in range(2)]
        small = ar.f32([128, 16])
        if p == 0:
            stT = ar.f32([128, DC, NS, 16])
            wsum = ar.f32([128, 4, NS])
        gs = groups(p)
        if p == 0:
            kb.op("dve", lambda e: e.memset(xT[:, :, 0:16], 0.0), writes=xkeys())
        else:
            kb.op("dve", lambda e: e.tensor_copy(out=xT[:, :, 1:16], in_=halo_save[j][:, :, 1:16]),
                  reads=["halo"], writes=xkeys())
        kb.op("dve", lambda e: e.tensor_copy(out=halo_save[j][:, :, 1:16], in_=xT[:, :, hi - 15:hi]),
              reads=xkeys(), writes=["halo"])
        if p == NPASS - 1:
            emit_rows_out(lambda c: halo_save[j][:, c, 1:16], 15, npp.ap()[j], 0)
        if p == 0:
            for t in range(2):
                st = stage[t]
                kb.op("sp", lambda e, st=st, t=t: e.dma_start(
                    out=st[0:120, :], in_=spool.ap()[j, t * 8:(t + 1) * 8].rearrange("s r d -> (s r) d")),
                    writes=[("stage", t)], dma=misc_sem())
                for q in range(4):
                    b = q % 2
                    for cc in range(4):
                        c = q * 4 + cc
                        kb.op("pe", lambda e, st=st, c=c, cc=cc, b=b: e.transpose(
                            out=ps[b][:, cc * 120:(cc + 1) * 120], in_=st[0:120, c * 128:(c + 1) * 128],
                            identity=cst[0:120, 0:120]),
                            reads=[("stage", t), "cst"], writes=[("ps", b)], inc=(cc == 3))
                    for cc in range(4):
                        kb.op("act", lambda e, q=q, cc=cc, b=b, t=t: e.copy(
                            out=stT[:, q * 4 + cc, t * 8:(t + 1) * 8, 0:15],
                            in_=ps[b][:, cc * 120:(cc + 1) * 120].rearrange("p (s r) -> p s r", s=8)),
                            reads=[("ps", b)], writes=["stT"])
            kb.op("dve", lambda e: e.tensor_copy(out=stT[:, :, :, 15], in_=xT[:, :, SC0:SC0 + NS]),
                  reads=xkeys() + ["stT"], writes=["stT"])
            kb.op("sp", lambda e: e.dma_start(out=nps.ap()[j, :, 0:14, :].rearrange("s r d -> s (r d)"),
                                              in_=spool.ap()[j, :, 1:15, :].rearrange("s r d -> s (r d)")),
                  dma=misc_sem())
            emit_rows_out(lambda c: xT[:, c, SC0:SC0 + NS], NS, nps.ap()[j, :, 14, :], 1)
        for c in range(DC):
            g = c // 4
            w = 2 ** (g + 1)
            cur = xT[:, c, :]
            ckey = [("xT", c)]
            for s in range(1, g + 2):
                sh = 2 ** (s - 1)
                lo = 2 ** s
                nxt = scr[(s - 1) % 2]
                kb.op("dve", lambda e, cur=cur, nxt=nxt, lo=lo, sh=sh: e.tensor_tensor(
                    out=nxt[:, lo:hi], in0=cur[:, lo:hi], in1=cur[:, lo - sh:hi - sh], op=ALU.add),
                    reads=ckey, writes=[("scr", (s - 1) % 2)])
                cur = nxt
                ckey = [("scr", (s - 1) % 2)]
            kb.op("dve", lambda e, cur=cur, c=c, w=w: e.scalar_tensor_tensor(
                out=xTb[:, c, C0:hi], in0=cur[:, C0:hi], scalar=1.0 / w, in1=xT[:, c, C0:hi],
                op0=ALU.mult, op1=ALU.subtract),
                reads=ckey + [("xT", c)], writes=[("xTb", c)])
            if p == 0:
                kb.op("dve", lambda e, cur=cur, g=g: e.tensor_tensor(
                    out=small[:, :], in0=cur[:, C0:C0 + 16], in1=invc[:, g * 16:(g + 1) * 16], op=ALU.mult),
                    reads=ckey + ["cst"], writes=["small"])
                kb.op("dve", lambda e, c=c: e.tensor_tensor(
                    out=xTb[:, c, C0:C0 + 16], in0=small[:, :], in1=xT[:, c, C0:C0 + 16], op=ALU.subtract),
                    reads=["small", ("xT", c)], writes=[("xTb", c)])
        if p == 0:
            for g in range(4):
                w = 2 ** (g + 1)
                kb.op("dve", lambda e, g=g, w=w: e.tensor_reduce(
                    out=wsum[:, :, :], in_=stT[:, 4 * g:4 * g + 4, :, 16 - w:16], axis=AX.X, op=ALU.add),
                    reads=["stT"], writes=["wsum"])
                kb.op("dve", lambda e, g=g, w=w: e.scalar_tensor_tensor(
                    out=xTb[:, 4 * g:4 * g + 4, SC0:SC0 + NS], in0=wsum[:, :, :], scalar=1.0 / w,
                    in1=xT[:, 4 * g:4 * g + 4, SC0:SC0 + NS], op0=ALU.mult, op1=ALU.subtract),
                    reads=["wsum"] + [("xT", 4 * g + k) for k in range(4)], writes=[("xTb", 4 * g + k) for k in range(4)])
        emit_fill(cfg.get("pool_fill", 0))
        pb = [0]
        for g in range(4):
            sA = load_WA(pool_w.ap()[j, g].rearrange("(c p) n -> p c n", p=128), 4, 512)
            Wv = wa_view(sA, 4, 512)
            for m in range(4):
                for gi, (g0, gn) in enumerate(gs):
                    bank = 4 + 2 * gi + (pb[0] % 2)
                    for ci in range(4):
                        kb.op("pe", lambda e, Wv=Wv, ci=ci, m=m, g=g, g0=g0, gn=gn, bank=bank: e.matmul(
                            ps[bank][:, 0:gn], lhsT=Wv[:, ci, m * 128:(m + 1) * 128],
                            rhs=xTb[:, 4 * g + ci, g0:g0 + gn], start=(ci == 0), stop=(ci == 3)),
                            reads=[("WA", sA), ("xTb", 4 * g + ci)], writes=[("ps", bank)], inc=(ci == 3))
                    cch = 4 * g + m
                    scol = ROW_PS + j * 16 + cch
                    kb.op("dve", lambda e, cch=cch, g0=g0, gn=gn, bank=bank, scol=scol: e.scalar_tensor_tensor(
                        out=xT[:, cch, g0:g0 + gn], in0=ps[bank][:, 0:gn], scalar=cols[:, scol:scol + 1],
                        in1=xT[:, cch, g0:g0 + gn], op0=ALU.mult, op1=ALU.add),
                        reads=[("ps", bank), ("xT", cch), "cols"] + [("xTb", 4 * g + k) for k in range(4)],
                        writes=[("xT", cch)])
                pb[0] += 1

    def emit_gla(p, l):
        j = l // 2
        kb.barrier()
        ar.reset()
        gs = groups(p)
        ntt = NTT + (1 if p == 0 else 0)
        qTs = [ar.bf16([128, 2, COLS]) for _ in range(2)]
        kTs = [ar.bf16([128, 2, COLS]) for _ in range(2)]
        vbs = [ar.bf16([128, NTT + 1, 512]), stage[0].bitcast(BF16)[:, 0:(NTT + 1) * 512].rearrange("p (a b) -> p a b", a=NTT + 1)]
        sogs = [ar.bf16([128, NTT + 1, 512]), stage[1].bitcast(BF16)[:, 0:(NTT + 1) * 512].rearrange("p (a b) -> p a b", a=NTT + 1)]
        onT = ar.bf16([128, 4, COLS])
        gkl = ar.f32([128, COLS])
        Sb = ar.bf16([128, 2, 512])
        gkts = [ar.f32([128, 256]) for _ in range(2)]
        ebTs = [ar.f32([128, 256]) for _ in range(2)]
        enbTs = [ar.f32([128, 256]) for _ in range(2)]
        qtls = [ar.bf16([128, 2, 128]) for _ in range(2)]
        ktls = [ar.bf16([128, 2, 128]) for _ in range(2)]
        ktoks = [ar.bf16([128, 256]) for _ in range(2)]
        ATbs = [ar.bf16([128, 128]) for _ in range(2)]
        gkt, ktok = gkts[0], ktoks[0]
        tmp = ar.f32([128, 512])
        onb = ar.bf16([128, 512])
        ssq = ar.f32([128, 2])
        if p == 0:
            aT = ar.f32([128, 32])
            kTf = ar.f32([128, 2, NS])
            Qm = ar.bf16([128, 2, NS, NS])
            Km = [ar.bf16([128, 256]) for _ in range(2)]
            Snb = [ar.bf16([128, 2, 512]) for _ in range(2)]
        kb.op("dve", lambda e: e.memset(ssq[:, :], 0.0), reads=[],
              writes=["ssq", ("stage", 0), ("stage", 1), ("vb", 1), ("sog", 1)])
        kb.op("sp", lambda e: e.dma_start(out=wgu[:], in_=d_wgu.ap()[j]), writes=["wgu"], dma=misc_sem())
        kb.op("sp", lambda e: e.dma_start(out=gbias_bc[:], in_=d_gbias.ap()[j:j + 1, :].partition_broadcast(128)[:, 0, :]),
              writes=["gbias"], dma=misc_sem())
        kb.op("sp", lambda e: e.dma_start(out=normw_bc[:], in_=d_normw.ap()[j:j + 1, :].partition_broadcast(128)[:, 0, :]),
              writes=["normw"], dma=misc_sem())
        sA = load_WA(w_in.ap()[j, :, 6144:6160].rearrange("(c p) r -> p c r", p=128), DC, 16)
        Wv = wa_view(sA, DC, 16)
        for gi, (g0, gn) in enumerate(gs):
            for dc in range(DC):
                kb.op("pe", lambda e, Wv=Wv, dc=dc, g0=g0, gn=gn: e.matmul(
                    ps[2][0:16, 0:gn], lhsT=Wv[:, dc, 0:16], rhs=xTb[:, dc, g0:g0 + gn],
                    start=(dc == 0), stop=(dc == DC - 1)),
                    reads=[("WA", sA), ("xTb", dc)], writes=[("ps", 2)], inc=(dc == DC - 1))
            kb.op("act", lambda e, g0=g0, gn=gn: e.copy(out=gkl[0:16, g0:g0 + gn], in_=ps[2][0:16, 0:gn]),
                  reads=[("ps", 2)], writes=["gkl"])

        pjb = [0]

        def proj_steps(h):
            hb = h % 2
            qT, kT, vb, sog = qTs[hb], kTs[hb], vbs[hb], sogs[hb]
            steps = []
            st = {}
            for base, dst, scl, nm in ((h * 256, qT, 1.0 / 16, "q"), (1024 + h * 256, kT, 1.0, "k")):
                for kc in range(2):
                    for gi, (g0, gn) in enumerate(gs):
                        def step(base=base, dst=dst, scl=scl, nm=nm, kc=kc, gi=gi, g0=g0, gn=gn):
                            if kc == 0 and gi == 0:
                                st["sA"] = load_WA(w_in.ap()[j, :, base:base + 256].rearrange("(c p) f -> p c f", p=128), DC, 256)
                            sA = st["sA"]
                            Wv = wa_view(sA, DC, 256)
                            bank = pjb[0] % 2
                            pjb[0] += 1
                            for dc in range(DC):
                                kb.op("pe", lambda e, dc=dc: e.matmul(
                                    ps[bank][:, 0:gn], lhsT=Wv[:, dc, kc * 128:(kc + 1) * 128],
                                    rhs=xTb[:, dc, g0:g0 + gn], start=(dc == 0), stop=(dc == DC - 1)),
                                    reads=[("WA", sA), ("xTb", dc)], writes=[("ps", bank)], inc=(dc == DC - 1))
                            kb.op("act", lambda e: e.activation(
                                out=dst[:, kc, g0:g0 + gn], in_=ps[bank][:, 0:gn], func=AF.Identity, scale=scl),
                                reads=[("ps", bank)], writes=[("qk", hb)])
                            if nm == "k" and gi == len(gs) - 1 and p == 0 and False:
                                pass
                        steps.append(step)
            for which, base in (("v", 2048 + h * 512), ("og", 4096 + h * 512)):
                for half in range(2):
                    for tt in range(ntt):
                        def step(which=which, base=base, half=half, tt=tt):
                            if tt == 0:
                                st["sA"] = load_WA(w_in.ap()[j, :, base + half * 256:base + half * 256 + 256].rearrange(
                                    "(c p) f -> p c f", p=128), DC, 256)
                            sA = st["sA"]
                            Wv = wa_view(sA, DC, 256)
                            c0, M = (C0 + tt * 128, 128) if tt < NTT else (SC0, NS)
                            bank = pjb[0] % 2
                            pjb[0] += 1
                            for dc in range(DC):
                                kb.op("pe", lambda e, dc=dc: e.matmul(
                                    ps[bank][0:M, 0:256], lhsT=xTb[:, dc, c0:c0 + M], rhs=Wv[:, dc, 0:256],
                                    start=(dc == 0), stop=(dc == DC - 1)),
                                    reads=[("WA", sA), ("xTb", dc)], writes=[("ps", bank)], inc=(dc == DC - 1))
                            if which == "v":
                                kb.op("act", lambda e: e.copy(
                                    out=vb[0:M, tt, half * 256:(half + 1) * 256], in_=ps[bank][0:M, 0:256]),
                                    reads=[("ps", bank)], writes=[("vb", hb)])
                            else:
                                kb.op("act", lambda e: e.activation(
                                    out=sog[0:M, tt, half * 256:(half + 1) * 256], in_=ps[bank][0:M, 0:256], func=AF.Silu),
                                    reads=[("ps", bank)], writes=[("sog", hb)])
                        steps.append(step)
            return steps

        def chain_steps(h):
            hb = h % 2
            qT, kT, vb, sog = qTs[hb], kTs[hb], vbs[hb], sogs[hb]
            QK, VB, SOG = ("qk", hb), ("vb", hb), ("sog", hb)
            steps = []
            A = steps.append

            def o_epilogue_steps(M, obank, tt, c0):
                def s1():
                    kb.op("act", lambda e: e.activation(out=tmp[0:M, :], in_=ps[obank][0:M, :], func=AF.Square,
                                                        accum_out=ssq[0:M, 0:1]),
                          reads=[("ps", obank)], writes=["tmp", "ssq"])
                    kb.op("act", lambda e: e.activation(out=ssq[0:M, 0:1], in_=ssq[0:M, 0:1], func=AF.Sqrt,
                                                        bias=epsc[0:M, 1:2], scale=1.0 / 512),
                          reads=["ssq", "epsc"], writes=["ssq"])
                    kb.op("dve", lambda e: e.reciprocal(out=ssq[0:M, 0:1], in_=ssq[0:M, 0:1]), reads=["ssq"], writes=["ssq"])
                    kb.op("dve", lambda e: e.scalar_tensor_tensor(out=tmp[0:M, :], in0=ps[obank][0:M, :], scalar=ssq[0:M, 0:1],
                                                                  in1=normw_bc[0:M, :], op0=ALU.mult, op1=ALU.mult),
                          reads=[("ps", obank), "ssq", "normw"], writes=["tmp"])
                    kb.op("dve", lambda e: e.tensor_tensor(out=onb[0:M, :], in0=tmp[0:M, :], in1=sog[0:M, tt, :], op=ALU.mult),
                          reads=["tmp", SOG], writes=["onb"])
                A(s1)

                def s2():
                    for vc in range(4):
                        kb.op("pe", lambda e, vc=vc: e.transpose(out=psb[4][:, vc * M:(vc + 1) * M],
                                                                  in_=onb[0:M, vc * 128:(vc + 1) * 128],
                                                                  identity=identb[0:M, 0:M]),
                              reads=["onb", "identb"], writes=[("ps", 4)], inc=(vc == 3))
                    kb.op("act", lambda e: e.copy(out=onT[:, :, c0:c0 + M],
                                                  in_=psb[4][:, 0:4 * M].rearrange("p (a t) -> p a t", a=4)),
                          reads=[("ps", 4)], writes=[("onT", k) for k in range(4)])
                A(s2)

            def s_init():
                if p == 0:
                    kb.op("dve", lambda e: e.memset(Sst[:], 0.0), writes=["S"])
                else:
                    kb.op("sp", lambda e: e.dma_start(out=Sst[:], in_=ngp.ap()[j, h].rearrange("(c p) v -> p c v", p=128)),
                          reads=["ngp"], writes=["S"], dma=misc_sem())
                kb.op("act", lambda e: e.copy(out=Sb[:, :, :], in_=Sst[:]), reads=["S"], writes=["Sb"])
            A(s_init)
            fronts, backs = [], []
            for tt in range(NTT):
                c0 = C0 + tt * 128
                pr = tt % 2
                gk_, eb_, enb_, qtl, ktl, ktk, ATb = gkts[pr], ebTs[pr], enbTs[pr], qtls[pr], ktls[pr], ktoks[pr], ATbs[pr]
                KG, KE, KN, KQ, KK, KT, KA = ("gkt", pr), ("ebT", pr), ("enbT", pr), ("qtl", pr), ("ktl", pr), ("ktok", pr), ("ATb", pr)
                F, Bk = [], []

                def b1(c0=c0, gk_=gk_, KG=KG):
                    kb.op("pe", lambda e: e.matmul(ps[2][:, 0:256], lhsT=gkl[0:16, c0:c0 + 128],
                                                   rhs=wgu[0:16, h * 256:(h + 1) * 256], start=True, stop=True),
                          reads=["gkl", "wgu"], writes=[("ps", 2)])
                    kb.op("dve", lambda e: e.tensor_tensor(out=gk_[:, :], in0=ps[2][:, 0:256],
                                                           in1=gbias_bc[:, h * 256:(h + 1) * 256], op=ALU.add),
                          reads=[("ps", 2), "gbias"], writes=[KG])
                    kb.op("act", lambda e: e.activation(out=gk_[:, :], in_=gk_[:, :], func=AF.Exp, scale=-1.0),
                          reads=[KG], writes=[KG])
                    kb.op("act", lambda e: e.activation(out=gk_[:, :], in_=gk_[:, :], func=AF.Ln, bias=1.0, scale=1.0),
                          reads=[KG], writes=[KG])
                F.append(b1)

                def b2(c0=c0, gk_=gk_, eb_=eb_, enb_=enb_, qtl=qtl, ktl=ktl, KG=KG, KE=KE, KN=KN, KQ=KQ, KK=KK):
                    for kc in range(2):
                        kb.op("pe", lambda e, kc=kc: e.matmul(ps[3][:, kc * 128:(kc + 1) * 128],
                                                              lhsT=gk_[:, kc * 128:(kc + 1) * 128], rhs=Umask,
                                                              start=True, stop=True),
                              reads=[KG, "cst"], writes=[("ps", 3)], inc=(kc == 1))
                    kb.op("act", lambda e: e.activation(out=eb_[:, :], in_=ps[3][:, 0:256], func=AF.Exp, scale=-1.0 / 16),
                          reads=[("ps", 3)], writes=[KE])
                    kb.op("act", lambda e: e.activation(out=enb_[:, :], in_=ps[3][:, 0:256], func=AF.Exp, scale=1.0 / 16),
                          reads=[("ps", 3), KE], writes=[KN])
                    kb.op("dve", lambda e: e.tensor_tensor(out=qtl[:, :, :], in0=qT[:, :, c0:c0 + 128],
                                                           in1=eb_.rearrange("p (a t) -> p a t", a=2), op=ALU.mult),
                          reads=[QK, KE], writes=[KQ])
                    kb.op("dve", lambda e: e.tensor_tensor(out=ktl[:, :, :], in0=kT[:, :, c0:c0 + 128],
                                                           in1=enb_.rearrange("p (a t) -> p a t", a=2), op=ALU.mult),
                          reads=[QK, KN], writes=[KK])
                F.append(b2)

                def b3(qtl=qtl, ktl=ktl, ktk=ktk, ATb=ATb, KQ=KQ, KK=KK, KT=KT, KA=KA):
                    for kc in range(2):
                        kb.op("pe", lambda e, kc=kc: e.transpose(out=psb[3][:, 512 + kc * 128:512 + (kc + 1) * 128], in_=ktl[:, kc, :],
                                                                  identity=identb[:]),
                              reads=[KK, "identb"], writes=[("ps", 3)], inc=(kc == 1))
                    kb.op("act", lambda e: e.copy(out=ktk[:, :], in_=psb[3][:, 512:768]), reads=[("ps", 3)], writes=[KT])
                    for kc in range(2):
                        kb.op("pe", lambda e, kc=kc: e.matmul(ps[2][:, 256:384], lhsT=ktl[:, kc, :], rhs=qtl[:, kc, :],
                                                              start=(kc == 0), stop=(kc == 1)),
                              reads=[KK, KQ], writes=[("ps", 2)], inc=(kc == 1))
                    kb.op("dve", lambda e: e.tensor_tensor(out=ATb[:, :], in0=ps[2][:, 256:384], in1=Umask, op=ALU.mult),
                          reads=[("ps", 2), "cst"], writes=[KA])
                F.append(b3)

                def b4(tt=tt, eb_=eb_, qtl=qtl, ktk=ktk, ATb=ATb, KE=KE, KQ=KQ, KT=KT, KA=KA):
                    kb.op("pe", lambda e: e.matmul(ps[5][:, :], lhsT=ATb[:, :], rhs=vb[:, tt, :], start=True, stop=False),
                          reads=[KA, VB], writes=[("ps", 5)], inc=False)
                    for kc in range(2):
                        kb.op("pe", lambda e, kc=kc: e.matmul(ps[5][:, :], lhsT=qtl[:, kc, :], rhs=Sb[:, kc, :],
                                                              start=False, stop=(kc == 1)),
                              reads=[KQ, "Sb"], writes=[("ps", 5)], inc=(kc == 1))
                    for kc in range(2):
                        kb.op("pe", lambda e, kc=kc: e.matmul(ps[6 + kc][:, :], lhsT=ktk[:, kc * 128:(kc + 1) * 128],
                                                              rhs=vb[:, tt, :], start=True, stop=True),
                              reads=[KT, VB], writes=[("ps", 6 + kc)])
                        kb.op("dve", lambda e, kc=kc: e.tensor_tensor(out=Sst[:, kc, :], in0=ps[6 + kc][:, :],
                                                                      in1=Sst[:, kc, :], op=ALU.add),
                              reads=[("ps", 6 + kc), "S"], writes=["S"])
                        kb.op("dve", lambda e, kc=kc: e.tensor_scalar(out=Sst[:, kc, :], in0=Sst[:, kc, :],
                                                                      scalar1=eb_[:, kc * 128 + 127:kc * 128 + 128],
                                                                      scalar2=None, op0=ALU.mult),
                              reads=["S", KE], writes=["S"])
                    kb.op("act", lambda e: e.copy(out=Sb[:, :, :], in_=Sst[:]), reads=["S"], writes=["Sb"])
                Bk.append(b4)
                n0 = len(steps)
                o_epilogue_steps(128, 5, tt, c0)
                Bk.extend(steps[n0:])
                del steps[n0:]
                fronts.append(F)
                backs.append(Bk)
            steps.extend(fronts[0])
            for tt in range(NTT):
                nxt = fronts[tt + 1] if tt + 1 < NTT else []
                bk = backs[tt]
                for i in range(max(len(bk), len(nxt))):
                    if i < len(bk):
                        steps.append(bk[i])
                    if i < len(nxt):
                        steps.append(nxt[i])

            def s_fin():
                kb.op("sp", lambda e: e.dma_start(out=ngp.ap()[j, h].rearrange("(c p) v -> p c v", p=128), in_=Sst[:]),
                      reads=["S"], writes=["ngp"], dma=misc_sem())
            A(s_fin)
            if p == 0:
                tt = NTT

                def c1():
                    kb.op("pe", lambda e: e.matmul(ps[2][0:NS, 0:256], lhsT=gkl[0:16, SC0:SC0 + NS],
                                                   rhs=wgu[0:16, h * 256:(h + 1) * 256], start=True, stop=True),
                          reads=["gkl", "wgu"], writes=[("ps", 2)])
                    kb.op("dve", lambda e: e.tensor_tensor(out=gkt[0:NS, :], in0=ps[2][0:NS, 0:256],
                                                           in1=gbias_bc[0:NS, h * 256:(h + 1) * 256], op=ALU.add),
                          reads=[("ps", 2), "gbias"], writes=[("gkt", 0)])
                    kb.op("act", lambda e: e.activation(out=gkt[0:NS, :], in_=gkt[0:NS, :], func=AF.Exp, scale=-1.0),
                          reads=[("gkt", 0)], writes=[("gkt", 0)])
                    kb.op("act", lambda e: e.activation(out=gkt[0:NS, :], in_=gkt[0:NS, :], func=AF.Ln, bias=1.0, scale=1.0),
                          reads=[("gkt", 0)], writes=[("gkt", 0)])
                A(c1)

                def c2():
                    for kc in range(2):
                        kb.op("pe", lambda e, kc=kc: e.matmul(ps[3][:, kc * NS:(kc + 1) * NS],
                                                              lhsT=gkt[0:NS, kc * 128:(kc + 1) * 128], rhs=cst[0:NS, 0:NS],
                                                              start=True, stop=True),
                              reads=[("gkt", 0), "cst"], writes=[("ps", 3)], inc=(kc == 1))
                    kb.op("act", lambda e: e.activation(out=aT[:, :], in_=ps[3][:, 0:2 * NS], func=AF.Exp, scale=-1.0 / 16),
                          reads=[("ps", 3)], writes=["aT"])
                    kb.op("dve", lambda e: e.tensor_copy(out=kTf[:, :, :], in_=kT[:, :, SC0:SC0 + NS]), reads=[QK], writes=["kTf"])
                A(c2)

                def c3():
                    for kc in range(2):
                        kb.op("pe", lambda e, kc=kc: e.transpose(out=ps[3][0:NS, 256 + kc * 128:256 + (kc + 1) * 128],
                                                                  in_=kTf[:, kc, :], identity=ident),
                              reads=["kTf", "cst"], writes=[("ps", 3)], inc=(kc == 1))
                    kb.op("act", lambda e: e.copy(out=ktok[0:NS, :], in_=ps[3][0:NS, 256:512]), reads=[("ps", 3)], writes=[("ktok", 0)])
                    for kc in range(2):
                        kb.op("dve", lambda e, kc=kc: e.tensor_tensor(
                            out=Qm[:, kc, :, :], in0=qT[:, kc, SC0:SC0 + NS].unsqueeze(2).to_broadcast([128, NS, NS]),
                            in1=eyeb, op=ALU.mult),
                            reads=[QK, "cst"], writes=["Qm"])
                A(c3)
                for s_ in range(NS):
                    def d1(s=s_):
                        b2_ = s % 2
                        kb.op("sp", lambda e: e.dma_start(
                            out=Ss[b2_][:], in_=sgla.ap()[j, s, h].rearrange("(c p) v -> p c v", p=128)),
                            writes=[("Ss", b2_)], dma="ss%d" % b2_)
                        kb.op("dve", lambda e: e.tensor_scalar(out=Km[b2_][0:NS, :], in0=ktok[0:NS, :],
                                                               scalar1=cst[0:NS, s:s + 1], scalar2=None, op0=ALU.mult),
                              reads=[("ktok", 0), "cst"], writes=[("Km", b2_)])
                        for kc in range(2):
                            kb.op("pe", lambda e, kc=kc: e.matmul(ps[6 + kc][:, :], lhsT=Km[b2_][0:NS, kc * 128:(kc + 1) * 128],
                                                                  rhs=vb[0:NS, tt, :], start=True, stop=True),
                                  reads=[("Km", b2_), VB], writes=[("ps", 6 + kc)])
                            kb.op("dve", lambda e, kc=kc: e.scalar_tensor_tensor(
                                out=Sn[b2_][:, kc, :], in0=Ss[b2_][:, kc, :], scalar=aT[:, kc * NS + s:kc * NS + s + 1],
                                in1=ps[6 + kc][:, :], op0=ALU.mult, op1=ALU.add),
                                reads=[("Ss", b2_), "aT", ("ps", 6 + kc)], writes=[("Sn", b2_)])
                        kb.op("sp", lambda e: e.dma_start(
                            out=ngs.ap()[j, s, h].rearrange("(c p) v -> p c v", p=128), in_=Sn[b2_][:]),
                            reads=[("Sn", b2_)], dma="sn%d" % b2_)
                        kb.op("act", lambda e: e.copy(out=Snb[b2_][:, :, :], in_=Sn[b2_][:]),
                              reads=[("Sn", b2_)], writes=[("Snb", b2_)])
                    A(d1)

                    def d2(s=s_):
                        b2_ = s % 2
                        for kc in range(2):
                            kb.op("pe", lambda e, kc=kc: e.matmul(
                                ps[4][0:NS, :], lhsT=Qm[:, kc, s, :], rhs=Snb[b2_][:, kc, :],
                                start=(s == 0 and kc == 0), stop=(s == NS - 1 and kc == 1)),
                                reads=["Qm", ("Snb", b2_)], writes=[("ps", 4)], inc=(kc == 1))
                    A(d2)
                o_epilogue_steps(NS, 4, tt, SC0)
            return steps

        def wout(h):
            sB = load_WB(w_out.ap()[j, h * 512:(h + 1) * 512, :].rearrange("(c p) d -> p c d", p=128))
            emit_accum_mm(gs, sB, onT, "onT", 4)

        for st_ in proj_steps(0):
            st_()
        for h in range(4):
            cs = chain_steps(h)
            psn = proj_steps(h + 1) if h < 3 else []
            ci = 0
            for pi, pstep in enumerate(psn):
                pstep()
                want = (len(cs) * (pi + 1)) // max(len(psn), 1)
                while ci < want:
                    cs[ci]()
                    ci += 1
            while ci < len(cs):
                cs[ci]()
                ci += 1
            wout(h)
        kb.op("dve", lambda e: e.memset(ssq[:, :], 0.0), reads=[],
              writes=["ssq", ("stage", 0), ("stage", 1), ("vb", 1), ("sog", 1)])

    for p in range(NPASS):
        emit_in(p)
        for l in range(NL):
            if l % 2 == 0:
                if MIX in ("pool", "all"):
                    emit_pool(p, l)
            else:
                if MIX == "all":
                    emit_gla(p, l)
            if cfg.get("do_ln", True):
                emit_ln(p, ROW_GM + l * 16, ROW_BM + l * 16, barrier=not (l % 2 == 0 and p > 0 and MIX != "none"))
            if cfg.get("do_mlp", True):
                emit_mlp(p, l)
            if cfg.get("do_ln", True):
                emit_ln(p, ROW_GF + l * 16, ROW_BF + l * 16, barrier=False)
        emit_out(p)

    kb.emit()
    return nc


def make_consts():
    c = np.zeros((128, CW), np.float32)
    c[:, 0:128] = np.eye(128, dtype=np.float32)
    j = np.arange(128)[:, None]
    i = np.arange(128)[None, :]
    c[:, 128:256] = (j <= i).astype(np.float32)
    for g in range(4):
        w = 2 ** (g + 1)
        t = np.arange(16)
        c[:, 256 + g * 16:256 + (g + 1) * 16] = (1.0 / np.minimum(t + 1, w))[None, :]
    c[:, 320:576] = np.eye(16, dtype=np.float32).reshape(1, 256)
    return c


def make_vecs(inp):
    rows = [
        np.asarray(inp["mlp_b1"], np.float32).reshape(-1, 128),
        np.asarray(inp["mlp_b2"], np.float32).reshape(-1, 128),
        np.asarray(inp["ln_mix_g"], np.float32).reshape(-1, 128),
        np.asarray(inp["ln_mix_b"], np.float32).reshape(-1, 128),
        np.asarray(inp["ln_ffn_g"], np.float32).reshape(-1, 128),
        np.asarray(inp["ln_ffn_b"], np.float32).reshape(-1, 128),
        np.asarray(inp["pool_scale"], np.float32).reshape(-1, 128),
    ]
    v = np.concatenate(rows, axis=0)
    out = np.zeros((NROWS_PAD, 128), np.float32)
    out[: v.shape[0]] = v
    return out


N_CORES = 8
PROMPT_CORES = (0, 1, 4, 5)
TP_RUN = 512
_NC_CACHE = {}


def kernel(x_prompt, x_sample, state_pool, state_gla, pool_w, pool_scale, gla_w_in, gla_w_gate_up,
           gla_gate_bias, gla_norm_w, gla_w_out, ln_mix_g, ln_mix_b, mlp_w1, mlp_b1, mlp_w2, mlp_b2,
           ln_ffn_g, ln_ffn_b):
    f = lambda a: np.ascontiguousarray(np.asarray(a), dtype=np.float32)
    x_prompt = f(x_prompt)
    x_sample = f(x_sample)
    state_pool = f(state_pool)
    state_gla = f(state_gla)
    B, L, _ = x_prompt.shape
    npass = L // TP_RUN
    key = (npass, TP_RUN)
    if key not in _NC_CACHE:
        _NC_CACHE[key] = build(dict(npass=npass, tp=TP_RUN, layers=DEPTH, mixers="all"))
    nc = _NC_CACHE[key]
    vec_in = dict(mlp_b1=mlp_b1, mlp_b2=mlp_b2, ln_mix_g=ln_mix_g, ln_mix_b=ln_mix_b, ln_ffn_g=ln_ffn_g,
                  ln_ffn_b=ln_ffn_b, pool_scale=pool_scale)
    shared = dict(vecs=make_vecs(vec_in), consts=make_consts(), mlp_w1=f(mlp_w1), mlp_w2=f(mlp_w2),
                  pool_w=f(pool_w), gla_w_in=f(gla_w_in), gla_wgu=f(gla_w_gate_up), gla_gbias=f(gla_gate_bias),
                  gla_normw=f(gla_norm_w), gla_w_out=f(gla_w_out))
    in_maps = []
    real = {PROMPT_CORES[b]: b for b in range(B)}
    zero_prompt = np.zeros_like(x_prompt[0])
    for c in range(N_CORES):
        m = dict(shared)
        m["xp"] = x_prompt[real[c]] if c in real else zero_prompt
        m["xs"] = x_sample[c * NS:(c + 1) * NS, 0]
        m["spool"] = np.ascontiguousarray(state_pool[:, c * NS:(c + 1) * NS])
        m["sgla"] = np.ascontiguousarray(state_gla[:, c * NS:(c + 1) * NS])
        in_maps.append(m)
    res = run_bass_kernel_spmd(nc, in_maps, core_ids=list(range(N_CORES)))
    r = res.results
    pc = PROMPT_CORES
    y_prompt = np.stack([r[pc[b]]["yp"] for b in range(B)], axis=0)
    y_sample = np.concatenate([r[c]["ys"] for c in range(N_CORES)], axis=0)[:, None, :]
    new_pool_prompt = np.stack([r[pc[b]]["npp"] for b in range(B)], axis=1)
    new_gla_prompt = np.stack([r[pc[b]]["ngp"] for b in range(B)], axis=1)
    new_pool_sample = np.concatenate([r[c]["nps"] for c in range(N_CORES)], axis=1)
    new_gla_sample = np.concatenate([r[c]["ngs"] for c in range(N_CORES)], axis=1)
    return (y_prompt.astype(np.float32), y_sample.astype(np.float32), new_pool_prompt.astype(np.float32),
            new_gla_prompt.astype(np.float32), new_pool_sample.astype(np.float32), new_gla_sample.astype(np.float32))
```

```python
import math
from contextlib import ExitStack

import numpy as np
import concourse.bass as bass
import concourse.mybir as mybir
from concourse.bass_utils import run_bass_kernel_spmd

F32 = mybir.dt.float32
BF16 = mybir.dt.bfloat16
AF = mybir.ActivationFunctionType
ALU = mybir.AluOpType
AX = mybir.AxisListType

D = 2048
DC = 16
DFF = 8192
FC = 64
DEPTH = 4
NS = 16
HALO = 16
ALPHA = (2 * DEPTH) ** 0.25
INV_ALPHA = 1.0 / ALPHA
LN_EPS = 1e-5
EPS_P = LN_EPS / (ALPHA * ALPHA)
GLA_IN = 6160
G_MLP = 4

ROW_B1 = 0
ROW_B2 = ROW_B1 + 4 * 64
ROW_GM = ROW_B2 + 4 * 16
ROW_BM = ROW_GM + 4 * 16
ROW_GF = ROW_BM + 4 * 16
ROW_BF = ROW_GF + 4 * 16
ROW_PS = ROW_BF + 4 * 16
NROWS = ROW_PS + 2 * 16
NROWS_PAD = 640


class KB:
    ENG = ("pe", "act", "dve", "pool", "sp")

    def __init__(self, nc):
        self.nc = nc
        self.ops = {e: [] for e in self.ENG}
        self.cnt = {}
        self.seen = {e: {} for e in self.ENG}
        self.trk = {}

    def op(self, eng, fn, reads=(), writes=(), inc=True, dma=None):
        deps = {}

        def add(d):
            for k, v in d.items():
                if deps.get(k, 0) < v:
                    deps[k] = v

        for key in reads:
            t = self.trk.get(key)
            if t:
                add(t["w"])
        for key in writes:
            t = self.trk.get(key)
            if t:
                add(t["w"])
                add(t["r"])
        waits = []
        for k, v in deps.items():
            if k == "pe" and eng == "pe":
                continue
            if self.seen[eng].get(k, 0) < v:
                self.seen[eng][k] = v
                waits.append((k, v))
        if dma is not None:
            prev = self.cnt.get(dma, 0)
            if prev and self.seen[eng].get(dma, 0) < prev:
                self.seen[eng][dma] = prev
                waits.append((dma, prev))
            self.cnt[dma] = prev + 16
            my = {dma: prev + 16}
            incr = (dma, 16)
        else:
            c = self.cnt.get(eng, 0)
            if inc:
                self.cnt[eng] = c + 1
                incr = (eng, 1)
            else:
                incr = None
            my = {eng: c + 1}
        for key in reads:
            t = self.trk.setdefault(key, {"w": {}, "r": {}})
            for k, v in my.items():
                if t["r"].get(k, 0) < v:
                    t["r"][k] = v
        for key in writes:
            self.trk[key] = {"w": dict(my), "r": {}}
        self.ops[eng].append((waits, fn, incr))

    def barrier(self, engs=("pe", "act", "dve")):
        for e in engs:
            waits = []
            for k in engs:
                v = self.cnt.get(k, 0)
                if k != e and v and self.seen[e].get(k, 0) < v:
                    self.seen[e][k] = v
                    waits.append((k, v))
            if waits:
                self.ops[e].append((waits, None, None))

    def emit(self, final_wait_eng="sp"):
        nc = self.nc
        keys = list(self.cnt.keys())
        with ExitStack() as es:
            sems = {}
            for i, k in enumerate(keys):
                sems[k] = es.enter_context(nc.semaphore("s%d" % i))
            fw = [(k, v) for k, v in self.cnt.items() if k not in self.ENG]
            fw += [(k, self.cnt[k]) for k in ("pe", "act", "dve") if k in self.cnt]
            self.ops[final_wait_eng].append((fw, None, None))

            def replay(name, e):
                for waits, fn, incr in self.ops[name]:
                    for k, v in waits:
                        e.wait_ge(sems[k], v)
                    if fn is None:
                        continue
                    ins = fn(e)
                    if incr is not None:
                        ins.then_inc(sems[incr[0]], incr[1])

            with nc.Block() as block:
                @block.tensor
                def _(e):
                    replay("pe", e)

                @block.scalar
                def _(e):
                    replay("act", e)

                @block.vector
                def _(e):
                    replay("dve", e)

                @block.gpsimd
                def _(e):
                    replay("pool", e)

                @block.sync
                def _(e):
                    replay("sp", e)


class Arena:
    def __init__(self, h, nwords):
        self.h = h
        self.hb = h.bitcast(BF16)
        self.n = nwords
        self.off = 0

    def reset(self):
        self.off = 0

    def _take(self, words):
        o = self.off
        self.off += words
        assert self.off <= self.n, ("arena overflow", self.off, self.n)
        return o

    @staticmethod
    def _shape(ap, shape):
        if len(shape) == 2:
            return ap
        if len(shape) == 3:
            return ap.rearrange("p (a b) -> p a b", a=shape[1])
        if len(shape) == 4:
            return ap.rearrange("p (a b c) -> p a b c", a=shape[1], b=shape[2])
        raise ValueError(shape)

    def f32(self, shape):
        n = int(np.prod(shape[1:]))
        o = self._take(n)
        return self._shape(self.h[:, o:o + n], shape)

    def bf16(self, shape):
        n = int(np.prod(shape[1:]))
        o = self._take((n + 1) // 2)
        return self._shape(self.hb[:, 2 * o:2 * o + n], shape)


CW = 576


def build(cfg):
    NPASS = cfg["npass"]
    TP = cfg["tp"]
    NL = cfg.get("layers", DEPTH)
    MIX = cfg.get("mixers", "all")
    NTOK = NPASS * TP
    COLS = HALO + TP + NS
    C0 = HALO
    SC0 = C0 + TP
    NTT = TP // 128

    nc = bass.Bass("TRN2", target_bir_lowering=False)
    kb = KB(nc)

    def din(name, shape):
        return nc.dram_tensor(name, list(shape), F32, kind="ExternalInput")

    def dout(name, shape):
        return nc.dram_tensor(name, list(shape), F32, kind="ExternalOutput")

    xp = din("xp", [NTOK, D])
    xs = din("xs", [NS, D])
    vecs = din("vecs", [NROWS_PAD, 128])
    consts = din("consts", [128, CW])
    w1 = din("mlp_w1", [DEPTH, D, DFF])
    w2 = din("mlp_w2", [DEPTH, DFF, D])
    yp = dout("yp", [NTOK, D])
    ys = dout("ys", [NS, D])
    if MIX != "none":
        spool = din("spool", [2, NS, 15, D])
        pool_w = din("pool_w", [2, 4, 512, 512])
        npp = dout("npp", [2, 15, D])
        nps = dout("nps", [2, NS, 15, D])
    if MIX == "all":
        sgla = din("sgla", [2, NS, 4, 256, 512])
        w_in = din("gla_w_in", [2, D, GLA_IN])
        d_wgu = din("gla_wgu", [2, 16, 1024])
        d_gbias = din("gla_gbias", [2, 1024])
        d_normw = din("gla_normw", [2, 512])
        w_out = din("gla_w_out", [2, D, D])
        ngp = dout("ngp", [2, 4, 256, 512])
        ngs = dout("ngs", [2, NS, 4, 256, 512])

    sb = lambda name, shape, dt=F32: nc.alloc_sbuf_tensor(name, list(shape), dt)
    xT = sb("xT", [128, DC, COLS])
    xTb = sb("xTb", [128, DC, COLS], BF16)
    cols = sb("cols", [128, NROWS_PAD])
    cst = sb("cst", [128, CW])
    identb = sb("identb", [128, 128], BF16)
    onesb = sb("onesb", [128, 128], BF16)
    epsc = sb("epsc", [128, 2])
    fillb = sb("fillb", [128, 512], BF16)
    stage = [sb("stage%d" % i, [128, D]) for i in range(2)]
    WA = [sb("WA%d" % i, [128, 4096], BF16) for i in range(3)]
    WB = [sb("WB%d" % i, [128, G_MLP, D], BF16) for i in range(2)]
    halo_save = [sb("halo%d" % i, [128, DC, 16]) for i in range(2)]
    if MIX == "all":
        wgu = sb("wgu", [16, 1024])
        gbias_bc = sb("gbias_bc", [128, 1024])
        normw_bc = sb("normw_bc", [128, 512])
        Ss = [sb("Ss%d" % i, [128, 2, 512]) for i in range(2)]
        Sn = [sb("Sn%d" % i, [128, 2, 512]) for i in range(2)]
        Sst = sb("Sst", [128, 2, 512])
    rem = int(nc.sbuf_bytes_remaining) - 256
    AW = rem // 4
    ar = Arena(sb("arena", [128, AW]), AW)

    ps = [nc.alloc_psum_tensor("ps%d" % i, [128, 512], F32) for i in range(8)]
    psb = [t.bitcast(BF16) for t in ps]
    ident = cst[:, 0:128]
    Umask = cst[:, 128:256]
    invc = cst[:, 256:320]
    eyeb = cst[:, 320:576].rearrange("p (a b) -> p a b", a=16)

    misc_i = [0]

    def misc_sem():
        misc_i[0] += 1
        return "m%d" % (misc_i[0] % 6)

    def wa_view(s, c, n):
        return WA[s][:, 0:c * n].rearrange("p (c n) -> p c n", c=c)

    kb.op("sp", lambda e: e.dma_start(out=cst[:], in_=consts.ap()), writes=["cst"], dma=misc_sem())
    kb.op("dve", lambda e: e.tensor_copy(out=identb[:], in_=cst[:, 0:128]), reads=["cst"], writes=["identb"])
    kb.op("dve", lambda e: e.memset(onesb[:], 1.0 / D), writes=["onesb"])
    kb.op("dve", lambda e: e.memset(fillb[:], 0.0), writes=["fillb"])
    kb.op("dve", lambda e: e.memset(epsc[:, 0:1], EPS_P), writes=["epsc"])
    kb.op("dve", lambda e: e.memset(epsc[:, 1:2], 1e-5), writes=["epsc"])
    for t in range(NROWS_PAD // 128):
        st = stage[t % 2]
        kb.op("sp", lambda e, st=st, t=t: e.dma_start(out=st[:, 0:128], in_=vecs.ap()[t * 128:(t + 1) * 128, :]),
              writes=[("stage", t % 2)], dma=misc_sem())
        kb.op("pe", lambda e, st=st: e.transpose(out=ps[0][:, 0:128], in_=st[:, 0:128], identity=ident),
              reads=[("stage", t % 2), "cst"], writes=[("ps", 0)])
        kb.op("act", lambda e, t=t: e.copy(out=cols[:, t * 128:(t + 1) * 128], in_=ps[0][:, 0:128]),
              reads=[("ps", 0)], writes=["cols"])
    kb.op("dve", lambda e: e.tensor_scalar(out=cols[:, ROW_B2:ROW_B2 + 64], in0=cols[:, ROW_B2:ROW_B2 + 64],
                                           scalar1=INV_ALPHA, scalar2=None, op0=ALU.mult),
          reads=["cols"], writes=["cols"])
    kb.op("dve", lambda e: e.tensor_scalar(out=cols[:, ROW_PS:ROW_PS + 32], in0=cols[:, ROW_PS:ROW_PS + 32],
                                           scalar1=INV_ALPHA, scalar2=None, op0=ALU.mult),
          reads=["cols"], writes=["cols"])

    wa_i = [0]
    wb_i = [0]

    def load_WA(src_ap, c, n):
        s = wa_i[0] % 3
        wa_i[0] += 1
        dst = wa_view(s, c, n)
        kb.op("pool", lambda e: e.dma_start(out=dst, in_=src_ap), writes=[("WA", s)], dma="wa%d" % s)
        return s

    def load_WB(src_ap):
        s = wb_i[0] % 2
        wb_i[0] += 1
        kb.op("pool", lambda e: e.dma_start(out=WB[s][:], in_=src_ap), writes=[("WB", s)], dma="wb%d" % s)
        return s

    def xkeys():
        return [("xT", i) for i in range(DC)]

    def xbkeys():
        return [("xTb", i) for i in range(DC)]

    def groups(p):
        g = [(C0 + i * 512, min(512, TP - i * 512)) for i in range((TP + 511) // 512)]
        if p == 0:
            g.append((SC0, NS))
        return g

    def emit_in(p):
        for i in range(NTT):
            st = stage[i % 2]
            r0 = p * TP + i * 128
            kb.op("sp", lambda e, st=st, r0=r0: e.dma_start(out=st[:], in_=xp.ap()[r0:r0 + 128, :]),
                  writes=[("stage", i % 2)], dma=misc_sem())
            for q in range(4):
                b = q % 2
                for cc in range(4):
                    c = q * 4 + cc
                    kb.op("pe", lambda e, st=st, c=c, cc=cc, b=b: e.transpose(
                        out=ps[b][:, cc * 128:(cc + 1) * 128], in_=st[:, c * 128:(c + 1) * 128], identity=ident),
                        reads=[("stage", i % 2), "cst"], writes=[("ps", b)], inc=(cc == 3))
                col = C0 + i * 128
                kb.op("act", lambda e, q=q, b=b, col=col: e.copy(
                    out=xT[:, q * 4:(q + 1) * 4, col:col + 128],
                    in_=ps[b][:].rearrange("p (a t) -> p a t", a=4)),
                    reads=[("ps", b)], writes=[("xT", q * 4 + k) for k in range(4)])
                kb.op("dve", lambda e, q=q, col=col: e.tensor_copy(
                    out=xTb[:, q * 4:(q + 1) * 4, col:col + 128],
                    in_=xT[:, q * 4:(q + 1) * 4, col:col + 128]),
                    reads=[("xT", q * 4 + k) for k in range(4)], writes=[("xTb", q * 4 + k) for k in range(4)])
        if p == 0:
            st = stage[0]
            kb.op("sp", lambda e: e.dma_start(out=st[0:NS, :], in_=xs.ap()), writes=[("stage", 0)], dma=misc_sem())
            for c in range(DC):
                kb.op("pe", lambda e, c=c: e.transpose(out=ps[2][:, c * NS:(c + 1) * NS],
                                                        in_=st[0:NS, c * 128:(c + 1) * 128],
                                                        identity=cst[0:NS, 0:NS]),
                      reads=[("stage", 0), "cst"], writes=[("ps", 2)], inc=(c == DC - 1))
            kb.op("act", lambda e: e.copy(out=xT[:, :, SC0:SC0 + NS],
                                          in_=ps[2][:, 0:DC * NS].rearrange("p (c s) -> p c s", c=DC)),
                  reads=[("ps", 2)], writes=xkeys())
            kb.op("dve", lambda e: e.tensor_copy(out=xTb[:, :, SC0:SC0 + NS], in_=xT[:, :, SC0:SC0 + NS]),
                  reads=xkeys(), writes=xbkeys())

    def emit_rows_out(src_of_chunk, M, dst_ap, sidx):
        st = stage[sidx]
        for q in range(4):
            for cc in range(4):
                c = q * 4 + cc
                kb.op("pe", lambda e, c=c, cc=cc: e.transpose(
                    out=ps[3][0:M, cc * 128:(cc + 1) * 128], in_=src_of_chunk(c), identity=ident),
                    reads=xkeys() + ["cst", "halo"], writes=[("ps", 3)], inc=(cc == 3))
            kb.op("act", lambda e, q=q: e.copy(out=st[0:M, q * 512:(q + 1) * 512], in_=ps[3][0:M, :]),
                  reads=[("ps", 3)], writes=[("stage", sidx)])
        kb.op("sp", lambda e: e.dma_start(out=dst_ap, in_=st[0:M, :]), reads=[("stage", sidx)], dma=misc_sem())

    def emit_out(p):
        for i in range(NTT):
            col = C0 + i * 128
            r0 = p * TP + i * 128
            emit_rows_out(lambda c, col=col: xT[:, c, col:col + 128], 128, yp.ap()[r0:r0 + 128, :], i % 2)
        if p == 0:
            emit_rows_out(lambda c: xT[:, c, SC0:SC0 + NS], NS, ys.ap(), 0)

    MLP_WORDS = 2 * ((G_MLP * COLS + 1) // 2) + 2 * (512 + NS)

    def emit_fill(n):
        for i in range(n):
            kb.op("pe", lambda e: e.matmul(ps[7][:, :], lhsT=fillb[:, 0:128], rhs=fillb[:, :], start=True, stop=True),
                  reads=["fillb"], writes=[("ps", 7)], inc=(i == n - 1))

    def emit_ln(p, grow, brow, barrier=True):
        if barrier:
            kb.barrier()
        ar.off = MLP_WORDS
        ln_mean = ar.f32([128, COLS])
        ln_rstd = ar.f32([128, COLS])
        ln_nmr = ar.f32([128, COLS])
        ln_zb = [ar.bf16([128, 4, COLS]) for _ in range(2)]
        ln_zq = [ar.bf16([128, 4, COLS]) for _ in range(2)]
        ln_u1 = ar.f32([128, 4, COLS])
        ln_u = [ln_u1, ln_u1]
        gs = groups(p)
        lo = C0
        hi = gs[-1][0] + gs[-1][1]
        n = hi - lo
        for q in range(4):
            b = q % 2
            xk = [("xT", q * 4 + k) for k in range(4)]
            kb.op("dve", lambda e, q=q, b=b: e.tensor_copy(out=ln_zb[b][:, :, lo:hi], in_=xT[:, q * 4:q * 4 + 4, lo:hi]),
                  reads=xk, writes=[("zb", b)])
            kb.op("act", lambda e, q=q, b=b: e.activation(out=ln_zq[b][:, :, lo:hi], in_=xT[:, q * 4:q * 4 + 4, lo:hi], func=AF.Square),
                  reads=xk, writes=[("zq", b)])
            for cc in range(4):
                c = q * 4 + cc
                for gi, (g0, gn) in enumerate(gs):
                    kb.op("pe", lambda e, b=b, cc=cc, g0=g0, gn=gn, gi=gi, c=c: e.matmul(
                        ps[gi][:, 0:gn], lhsT=onesb[:], rhs=ln_zb[b][:, cc, g0:g0 + gn], start=(c == 0), stop=(c == DC - 1)),
                        reads=[("zb", b), "onesb"], writes=[("ps", gi)], inc=False)
                    kb.op("pe", lambda e, b=b, cc=cc, g0=g0, gn=gn, gi=gi, c=c: e.matmul(
                        ps[4 + gi][:, 0:gn], lhsT=onesb[:], rhs=ln_zq[b][:, cc, g0:g0 + gn], start=(c == 0), stop=(c == DC - 1)),
                        reads=[("zq", b), "onesb"], writes=[("ps", 4 + gi)], inc=(cc == 3 and gi == len(gs) - 1))
        for gi, (g0, gn) in enumerate(gs):
            sl = slice(g0, g0 + gn)
            kb.op("act", lambda e, gi=gi, gn=gn, sl=sl: e.copy(out=ln_mean[:, sl], in_=ps[gi][:, 0:gn]),
                  reads=[("ps", gi)], writes=["ln_mean"])
            kb.op("dve", lambda e, sl=sl: e.tensor_tensor(out=ln_nmr[:, sl], in0=ln_mean[:, sl], in1=ln_mean[:, sl], op=ALU.mult),
                  reads=["ln_mean"], writes=["ln_nmr"])
            kb.op("dve", lambda e, gi=gi, gn=gn, sl=sl: e.tensor_tensor(out=ln_rstd[:, sl], in0=ps[4 + gi][:, 0:gn], in1=ln_nmr[:, sl], op=ALU.subtract),
                  reads=[("ps", 4 + gi), "ln_nmr"], writes=["ln_rstd"])
            kb.op("act", lambda e, sl=sl: e.activation(out=ln_rstd[:, sl], in_=ln_rstd[:, sl], func=AF.Sqrt, bias=epsc[:, 0:1], scale=1.0),
                  reads=["ln_rstd", "epsc"], writes=["ln_rstd"])
            kb.op("dve", lambda e, sl=sl: e.reciprocal(out=ln_rstd[:, sl], in_=ln_rstd[:, sl]),
                  reads=["ln_rstd"], writes=["ln_rstd"])
            kb.op("dve", lambda e, sl=sl: e.scalar_tensor_tensor(out=ln_nmr[:, sl], in0=ln_mean[:, sl], scalar=-1.0, in1=ln_rstd[:, sl], op0=ALU.mult, op1=ALU.mult),
                  reads=["ln_mean", "ln_rstd"], writes=["ln_nmr"])
        emit_fill(cfg.get("ln_fill", 0))
        rb = ln_rstd[:, lo:hi].unsqueeze(1).to_broadcast([128, 4, n])
        nb = ln_nmr[:, lo:hi].unsqueeze(1).to_broadcast([128, 4, n])
        for q in range(4):
            b = q % 2
            xk = [("xT", q * 4 + k) for k in range(4)]
            kb.op("dve", lambda e, q=q, b=b: e.tensor_tensor(out=ln_u[b][:, :, lo:hi], in0=xT[:, q * 4:q * 4 + 4, lo:hi], in1=rb, op=ALU.mult),
                  reads=xk + ["ln_rstd"], writes=["lnu"])
            kb.op("dve", lambda e, b=b: e.tensor_tensor(out=ln_u[b][:, :, lo:hi], in0=ln_u[b][:, :, lo:hi], in1=nb, op=ALU.add),
                  reads=["lnu", "ln_nmr"], writes=["lnu"])
            for cc in range(4):
                c = q * 4 + cc
                kb.op("act", lambda e, c=c, cc=cc, b=b: e.activation(out=xT[:, c, lo:hi], in_=ln_u[b][:, cc, lo:hi], func=AF.Identity,
                                                              bias=cols[:, brow + c:brow + c + 1], scale=cols[:, grow + c:grow + c + 1]),
                      reads=["lnu", "cols"], writes=[("xT", c)])
                kb.op("act", lambda e, c=c, cc=cc, b=b: e.activation(out=xTb[:, c, lo:hi], in_=ln_u[b][:, cc, lo:hi], func=AF.Identity,
                                                              bias=cols[:, brow + c:brow + c + 1], scale=cols[:, grow + c:grow + c + 1]),
                      reads=["lnu", "cols"], writes=[("xTb", c)])

    ybank = [0]

    def emit_accum_mm(gs, sB, src, skey, nk):
        for m in range(DC):
            for gi, (g0, gn) in enumerate(gs):
                bank = (4 + 2 * gi + (ybank[0] % 2)) if len(gs) > 1 else (4 + ybank[0] % 4)
                for k in range(nk):
                    kb.op("pe", lambda e, k=k, m=m, g0=g0, gn=gn, bank=bank: e.matmul(
                        ps[bank][:, 0:gn], lhsT=WB[sB][:, k, m * 128:(m + 1) * 128],
                        rhs=src[:, k, g0:g0 + gn], start=(k == 0), stop=(k == nk - 1)),
                        reads=[("WB", sB), (skey, k)], writes=[("ps", bank)], inc=(k == nk - 1))
                kb.op("dve", lambda e, m=m, g0=g0, gn=gn, bank=bank: e.scalar_tensor_tensor(
                    out=xT[:, m, g0:g0 + gn], in0=ps[bank][:, 0:gn], scalar=INV_ALPHA,
                    in1=xT[:, m, g0:g0 + gn], op0=ALU.mult, op1=ALU.add),
                    reads=[("ps", bank), ("xT", m)], writes=[("xT", m)])
            ybank[0] += 1

    def emit_mlp(p, l):
        ar.reset()
        hTs = [ar.bf16([128, G_MLP, COLS]) for _ in range(2)]
        rT = [ar.f32([128, 512 + NS]) for _ in range(2)]
        gs = groups(p)
        lo = C0
        hi = gs[-1][0] + gs[-1][1]
        for c in range(DC):
            kb.op("dve", lambda e, c=c: e.tensor_scalar(out=xT[:, c, lo:hi], in0=xT[:, c, lo:hi],
                                                        scalar1=cols[:, ROW_B2 + l * 16 + c:ROW_B2 + l * 16 + c + 1],
                                                        scalar2=None, op0=ALU.add),
                  reads=[("xT", c), "cols"], writes=[("xT", c)])
        st = {}

        def mm1(g):
            hT = hTs[g % 2]
            for jj in range(G_MLP):
                j = g * G_MLP + jj
                if j % 2 == 0:
                    f0 = j * 128
                    st["sA"] = load_WA(w1.ap()[l, :, f0:f0 + 256].rearrange("(c p) f -> p c f", p=128), DC, 256)
                sA = st["sA"]
                Wv = wa_view(sA, DC, 256)
                for gi, (g0, gn) in enumerate(gs):
                    bank = ((2 * gi) + (j % 2)) if len(gs) > 1 else (j % 4)
                    for dc in range(DC):
                        kb.op("pe", lambda e, Wv=Wv, j=j, dc=dc, g0=g0, gn=gn, bank=bank: e.matmul(
                            ps[bank][:, 0:gn], lhsT=Wv[:, dc, (j % 2) * 128:(j % 2) * 128 + 128],
                            rhs=xTb[:, dc, g0:g0 + gn], start=(dc == 0), stop=(dc == DC - 1)),
                            reads=[("WA", sA), ("xTb", dc)], writes=[("ps", bank)], inc=(dc == DC - 1))
                    rb = rT[j % 2]
                    roff = 0 if gi == 0 else 512
                    bcol = ROW_B1 + l * 64 + j
                    kb.op("act", lambda e, bank=bank, gn=gn, rb=rb, roff=roff, bcol=bcol: e.activation(
                        out=rb[:, roff:roff + gn], in_=ps[bank][:, 0:gn], func=AF.Relu,
                        bias=cols[:, bcol:bcol + 1], scale=1.0),
                        reads=[("ps", bank), "cols"], writes=[("rT", j % 2, gi)])
                    kb.op("act", lambda e, rb=rb, roff=roff, gn=gn, jj=jj, g0=g0, hT=hT: e.activation(
                        out=hT[:, jj, g0:g0 + gn], in_=rb[:, roff:roff + gn], func=AF.Square),
                        reads=[("rT", j % 2, gi)], writes=[(("hT", g % 2), jj)])

        def mm2(g):
            r0 = g * G_MLP * 128
            sB = load_WB(w2.ap()[l, r0:r0 + G_MLP * 128, :].rearrange("(j p) d -> p j d", p=128))
            emit_accum_mm(gs, sB, hTs[g % 2], ("hT", g % 2), G_MLP)

        NG = FC // G_MLP
        for g in range(NG):
            mm1(g)
            if g > 0:
                mm2(g - 1)
        mm2(NG - 1)

    def emit_pool(p, l):
        j = l // 2
        kb.barrier()
        ar.reset()
        hi = C0 + TP
        scr = [ar.f32([128, hi]) for _ in range(2)]
        small = ar.f32([128, 16])
        if p == 0:
            stT = ar.f32([128, DC, NS, 16])
            wsum = ar.f32([128, 4, NS])
        gs = groups(p)
        if p == 0:
            kb.op("dve", lambda e: e.memset(xT[:, :, 0:16], 0.0), writes=xkeys())
        else:
            kb.op("dve", lambda e: e.tensor_copy(out=xT[:, :, 1:16], in_=halo_save[j][:, :, 1:16]),
                  reads=["halo"], writes=xkeys())
        kb.op("dve", lambda e: e.tensor_copy(out=halo_save[j][:, :, 1:16], in_=xT[:, :, hi - 15:hi]),
              reads=xkeys(), writes=["halo"])
        if p == NPASS - 1:
            emit_rows_out(lambda c: halo_save[j][:, c, 1:16], 15, npp.ap()[j], 0)
        if p == 0:
            for t in range(2):
                st = stage[t]
                kb.op("sp", lambda e, st=st, t=t: e.dma_start(
                    out=st[0:120, :], in_=spool.ap()[j, t * 8:(t + 1) * 8].rearrange("s r d -> (s r) d")),
                    writes=[("stage", t)], dma=misc_sem())
                for q in range(4):
                    b = q % 2
                    for cc in range(4):
                        c = q * 4 + cc
                        kb.op("pe", lambda e, st=st, c=c, cc=cc, b=b: e.transpose(
                            out=ps[b][:, cc * 120:(cc + 1) * 120], in_=st[0:120, c * 128:(c + 1) * 128],
                            identity=cst[0:120, 0:120]),
                            reads=[("stage", t), "cst"], writes=[("ps", b)], inc=(cc == 3))
                    for cc in range(4):
                        kb.op("act", lambda e, q=q, cc=cc, b=b, t=t: e.copy(
                            out=stT[:, q * 4 + cc, t * 8:(t + 1) * 8, 0:15],
                            in_=ps[b][:, cc * 120:(cc + 1) * 120].rearrange("p (s r) -> p s r", s=8)),
                            reads=[("ps", b)], writes=["stT"])
            kb.op("dve", lambda e: e.tensor_copy(out=stT[:, :, :, 15], in_=xT[:, :, SC0:SC0 + NS]),
                  reads=xkeys() + ["stT"], writes=["stT"])
            kb.op("sp", lambda e: e.dma_start(out=nps.ap()[j, :, 0:14, :].rearrange("s r d -> s (r d)"),
                                              in_=spool.ap()[j, :, 1:15, :].rearrange("s r d -> s (r d)")),
                  dma=misc_sem())
            emit_rows_out(lambda c: xT[:, c, SC0:SC0 + NS], NS, nps.ap()[j, :, 14, :], 1)
        for c in range(DC):
            g = c // 4
            w = 2 ** (g + 1)
            cur = xT[:, c, :]
            ckey = [("xT", c)]
            for s in range(1, g + 2):
                sh = 2 ** (s - 1)
                lo = 2 ** s
                nxt = scr[(s - 1) % 2]
                kb.op("dve", lambda e, cur=cur, nxt=nxt, lo=lo, sh=sh: e.tensor_tensor(
                    out=nxt[:, lo:hi], in0=cur[:, lo:hi], in1=cur[:, lo - sh:hi - sh], op=ALU.add),
                    reads=ckey, writes=[("scr", (s - 1) % 2)])
                cur = nxt
                ckey = [("scr", (s - 1) % 2)]
            kb.op("dve", lambda e, cur=cur, c=c, w=w: e.scalar_tensor_tensor(
                out=xTb[:, c, C0:hi], in0=cur[:, C0:hi], scalar=1.0 / w, in1=xT[:, c, C0:hi],
                op0=ALU.mult, op1=ALU.subtract),
                reads=ckey + [("xT", c)], writes=[("xTb", c)])
            if p == 0:
                kb.op("dve", lambda e, cur=cur, g=g: e.tensor_tensor(
                    out=small[:, :], in0=cur[:, C0:C0 + 16], in1=invc[:, g * 16:(g + 1) * 16], op=ALU.mult),
                    reads=ckey + ["cst"], writes=["small"])
                kb.op("dve", lambda e, c=c: e.tensor_tensor(
                    out=xTb[:, c, C0:C0 + 16], in0=small[:, :], in1=xT[:, c, C0:C0 + 16], op=ALU.subtract),
                    reads=["small", ("xT", c)], writes=[("xTb", c)])
        if p == 0:
            for g in range(4):
                w = 2 ** (g + 1)
                kb.op("dve", lambda e, g=g, w=w: e.tensor_reduce(
                    out=wsum[:, :, :], in_=stT[:, 4 * g:4 * g + 4, :, 16 - w:16], axis=AX.X, op=ALU.add),
                    reads=["stT"], writes=["wsum"])
                kb.op("dve", lambda e, g=g, w=w: e.scalar_tensor_tensor(
                    out=xTb[:, 4 * g:4 * g + 4, SC0:SC0 + NS], in0=wsum[:, :, :], scalar=1.0 / w,
                    in1=xT[:, 4 * g:4 * g + 4, SC0:SC0 + NS], op0=ALU.mult, op1=ALU.subtract),
                    reads=["wsum"] + [("xT", 4 * g + k) for k in range(4)], writes=[("xTb", 4 * g + k) for k in range(4)])
        emit_fill(cfg.get("pool_fill", 0))
        pb = [0]
        for g in range(4):
            sA = load_WA(pool_w.ap()[j, g].rearrange("(c p) n -> p c n", p=128), 4, 512)
            Wv = wa_view(sA, 4, 512)
            for m in range(4):
                for gi, (g0, gn) in enumerate(gs):
                    bank = 4 + 2 * gi + (pb[0] % 2)
                    for ci in range(4):
                        kb.op("pe", lambda e, Wv=Wv, ci=ci, m=m, g=g, g0=g0, gn=gn, bank=bank: e.matmul(
                            ps[bank][:, 0:gn], lhsT=Wv[:, ci, m * 128:(m + 1) * 128],
                            rhs=xTb[:, 4 * g + ci, g0:g0 + gn], start=(ci == 0), stop=(ci == 3)),
                            reads=[("WA", sA), ("xTb", 4 * g + ci)], writes=[("ps", bank)], inc=(ci == 3))
                    cch = 4 * g + m
                    scol = ROW_PS + j * 16 + cch
                    kb.op("dve", lambda e, cch=cch, g0=g0, gn=gn, bank=bank, scol=scol: e.scalar_tensor_tensor(
                        out=xT[:, cch, g0:g0 + gn], in0=ps[bank][:, 0:gn], scalar=cols[:, scol:scol + 1],
                        in1=xT[:, cch, g0:g0 + gn], op0=ALU.mult, op1=ALU.add),
                        reads=[("ps", bank), ("xT", cch), "cols"] + [("xTb", 4 * g + k) for k in range(4)],
                        writes=[("xT", cch)])
                pb[0] += 1

    def emit_gla(p, l):
        j = l // 2
        kb.barrier()
        ar.reset()
        gs = groups(p)
        ntt = NTT + (1 if p == 0 else 0)
        qTs = [ar.bf16([128, 2, COLS]) for _ in range(2)]
        kTs = [ar.bf16([128, 2, COLS]) for _ in range(2)]
        vbs = [ar.bf16([128, NTT + 1, 512]), stage[0].bitcast(BF16)[:, 0:(NTT + 1) * 512].rearrange("p (a b) -> p a b", a=NTT + 1)]
        sogs = [ar.bf16([128, NTT + 1, 512]), stage[1].bitcast(BF16)[:, 0:(NTT + 1) * 512].rearrange("p (a b) -> p a b", a=NTT + 1)]
        onT = ar.bf16([128, 4, COLS])
        gkl = ar.f32([128, COLS])
        Sb = ar.bf16([128, 2, 512])
        gkt_all = ar.f32([128, NTT, 256])
        ebTs = [ar.f32([128, 256]) for _ in range(2)]
        enbTs = [ar.f32([128, 256]) for _ in range(2)]
        qtls = [ar.bf16([128, 2, 128]) for _ in range(2)]
        ktls = [ar.bf16([128, 2, 128]) for _ in range(2)]
        ktoks = [ar.bf16([128, 256]) for _ in range(2)]
        ATbs = [ar.bf16([128, 128]) for _ in range(2)]
        gkt, ktok = gkt_all[:, 0, :], ktoks[0]
        tmp = ar.f32([128, 512])
        onb = ar.bf16([128, 512])
        ssq = ar.f32([128, 2])
        if p == 0:
            aT = ar.f32([128, 32])
            kTf = ar.f32([128, 2, NS])
            Qm = ar.bf16([128, 2, NS, NS])
            Km = [ar.bf16([128, 256]) for _ in range(2)]
            Snb1 = ar.bf16([128, 2, 512])
            Snb = [Snb1, Snb1]
        kb.op("dve", lambda e: e.memset(ssq[:, :], 0.0), reads=[],
              writes=["ssq", ("stage", 0), ("stage", 1), ("vb", 1), ("sog", 1)])
        kb.op("sp", lambda e: e.dma_start(out=wgu[:], in_=d_wgu.ap()[j]), writes=["wgu"], dma=misc_sem())
        kb.op("sp", lambda e: e.dma_start(out=gbias_bc[:], in_=d_gbias.ap()[j:j + 1, :].partition_broadcast(128)[:, 0, :]),
              writes=["gbias"], dma=misc_sem())
        kb.op("sp", lambda e: e.dma_start(out=normw_bc[:], in_=d_normw.ap()[j:j + 1, :].partition_broadcast(128)[:, 0, :]),
              writes=["normw"], dma=misc_sem())
        sA = load_WA(w_in.ap()[j, :, 6144:6160].rearrange("(c p) r -> p c r", p=128), DC, 16)
        Wv = wa_view(sA, DC, 16)
        for gi, (g0, gn) in enumerate(gs):
            for dc in range(DC):
                kb.op("pe", lambda e, Wv=Wv, dc=dc, g0=g0, gn=gn: e.matmul(
                    ps[2][0:16, 0:gn], lhsT=Wv[:, dc, 0:16], rhs=xTb[:, dc, g0:g0 + gn],
                    start=(dc == 0), stop=(dc == DC - 1)),
                    reads=[("WA", sA), ("xTb", dc)], writes=[("ps", 2)], inc=(dc == DC - 1))
            kb.op("act", lambda e, g0=g0, gn=gn: e.copy(out=gkl[0:16, g0:g0 + gn], in_=ps[2][0:16, 0:gn]),
                  reads=[("ps", 2)], writes=["gkl"])

        pjb = [0]

        def proj_steps(h):
            hb = h % 2
            qT, kT, vb, sog = qTs[hb], kTs[hb], vbs[hb], sogs[hb]
            steps = []
            st = {}
            for base, dst, scl, nm in ((h * 256, qT, 1.0 / 16, "q"), (1024 + h * 256, kT, 1.0, "k")):
                for kc in range(2):
                    for gi, (g0, gn) in enumerate(gs):
                        def step(base=base, dst=dst, scl=scl, nm=nm, kc=kc, gi=gi, g0=g0, gn=gn):
                            if kc == 0 and gi == 0:
                                st["sA"] = load_WA(w_in.ap()[j, :, base:base + 256].rearrange("(c p) f -> p c f", p=128), DC, 256)
                            sA = st["sA"]
                            Wv = wa_view(sA, DC, 256)
                            bank = pjb[0] % 2
                            pjb[0] += 1
                            for dc in range(DC):
                                kb.op("pe", lambda e, dc=dc: e.matmul(
                                    ps[bank][:, 0:gn], lhsT=Wv[:, dc, kc * 128:(kc + 1) * 128],
                                    rhs=xTb[:, dc, g0:g0 + gn], start=(dc == 0), stop=(dc == DC - 1)),
                                    reads=[("WA", sA), ("xTb", dc)], writes=[("ps", bank)], inc=(dc == DC - 1))
                            kb.op("act", lambda e: e.activation(
                                out=dst[:, kc, g0:g0 + gn], in_=ps[bank][:, 0:gn], func=AF.Identity, scale=scl),
                                reads=[("ps", bank)], writes=[("qk", hb)])
                            if nm == "k" and gi == len(gs) - 1 and p == 0 and False:
                                pass
                        steps.append(step)
            for which, base in (("v", 2048 + h * 512), ("og", 4096 + h * 512)):
                for half in range(2):
                    for tt in range(ntt):
                        def step(which=which, base=base, half=half, tt=tt):
                            if tt == 0:
                                st["sA"] = load_WA(w_in.ap()[j, :, base + half * 256:base + half * 256 + 256].rearrange(
                                    "(c p) f -> p c f", p=128), DC, 256)
                            sA = st["sA"]
                            Wv = wa_view(sA, DC, 256)
                            c0, M = (C0 + tt * 128, 128) if tt < NTT else (SC0, NS)
                            bank = pjb[0] % 2
                            pjb[0] += 1
                            for dc in range(DC):
                                kb.op("pe", lambda e, dc=dc: e.matmul(
                                    ps[bank][0:M, 0:256], lhsT=xTb[:, dc, c0:c0 + M], rhs=Wv[:, dc, 0:256],
                                    start=(dc == 0), stop=(dc == DC - 1)),
                                    reads=[("WA", sA), ("xTb", dc)], writes=[("ps", bank)], inc=(dc == DC - 1))
                            if which == "v":
                                kb.op("act", lambda e: e.copy(
                                    out=vb[0:M, tt, half * 256:(half + 1) * 256], in_=ps[bank][0:M, 0:256]),
                                    reads=[("ps", bank)], writes=[("vb", hb)])
                            else:
                                kb.op("act", lambda e: e.activation(
                                    out=sog[0:M, tt, half * 256:(half + 1) * 256], in_=ps[bank][0:M, 0:256], func=AF.Silu),
                                    reads=[("ps", bank)], writes=[("sog", hb)])
                        steps.append(step)
            return steps

        def chain_steps(h):
            hb = h % 2
            qT, kT, vb, sog = qTs[hb], kTs[hb], vbs[hb], sogs[hb]
            QK, VB, SOG = ("qk", hb), ("vb", hb), ("sog", hb)
            steps = []
            A = steps.append

            def o_epilogue_steps(M, obank, tt, c0):
                def s1():
                    kb.op("act", lambda e: e.activation(out=tmp[0:M, :], in_=ps[obank][0:M, :], func=AF.Square,
                                                        accum_out=ssq[0:M, 0:1]),
                          reads=[("ps", obank)], writes=["tmp", "ssq"])
                    kb.op("act", lambda e: e.activation(out=ssq[0:M, 0:1], in_=ssq[0:M, 0:1], func=AF.Sqrt,
                                                        bias=epsc[0:M, 1:2], scale=1.0 / 512),
                          reads=["ssq", "epsc"], writes=["ssq"])
                    kb.op("dve", lambda e: e.reciprocal(out=ssq[0:M, 0:1], in_=ssq[0:M, 0:1]), reads=["ssq"], writes=["ssq"])
                    kb.op("dve", lambda e: e.scalar_tensor_tensor(out=tmp[0:M, :], in0=ps[obank][0:M, :], scalar=ssq[0:M, 0:1],
                                                                  in1=normw_bc[0:M, :], op0=ALU.mult, op1=ALU.mult),
                          reads=[("ps", obank), "ssq", "normw"], writes=["tmp"])
                    kb.op("dve", lambda e: e.tensor_tensor(out=onb[0:M, :], in0=tmp[0:M, :], in1=sog[0:M, tt, :], op=ALU.mult),
                          reads=["tmp", SOG], writes=["onb"])
                A(s1)

                def s2():
                    for vc in range(4):
                        kb.op("pe", lambda e, vc=vc: e.transpose(out=psb[4][:, vc * M:(vc + 1) * M],
                                                                  in_=onb[0:M, vc * 128:(vc + 1) * 128],
                                                                  identity=identb[0:M, 0:M]),
                              reads=["onb", "identb"], writes=[("ps", 4)], inc=(vc == 3))
                    kb.op("act", lambda e: e.copy(out=onT[:, :, c0:c0 + M],
                                                  in_=psb[4][:, 0:4 * M].rearrange("p (a t) -> p a t", a=4)),
                          reads=[("ps", 4)], writes=[("onT", k) for k in range(4)])
                A(s2)

            def s_init():
                if p == 0:
                    kb.op("dve", lambda e: e.memset(Sst[:], 0.0), writes=["S"])
                else:
                    kb.op("sp", lambda e: e.dma_start(out=Sst[:], in_=ngp.ap()[j, h].rearrange("(c p) v -> p c v", p=128)),
                          reads=["ngp"], writes=["S"], dma=misc_sem())
                kb.op("act", lambda e: e.copy(out=Sb[:, :, :], in_=Sst[:]), reads=["S"], writes=["Sb"])
            A(s_init)
            def gate_all():
                nb = (NTT + 1) // 2
                for tt in range(NTT):
                    c0 = C0 + tt * 128
                    bank = 6 + tt // 2
                    col = (tt % 2) * 256
                    kb.op("pe", lambda e, c0=c0, bank=bank, col=col: e.matmul(
                        ps[bank][:, col:col + 256], lhsT=gkl[0:16, c0:c0 + 128],
                        rhs=wgu[0:16, h * 256:(h + 1) * 256], start=True, stop=True),
                        reads=["gkl", "wgu"], writes=[("ps", bank)], inc=(tt % 2 == 1 or tt == NTT - 1))
                for b in range(nb):
                    nbk = min(2, NTT - 2 * b)
                    kb.op("dve", lambda e, b=b, nbk=nbk: e.tensor_tensor(
                        out=gkt_all[:, 2 * b:2 * b + nbk, :],
                        in0=ps[6 + b][:, 0:nbk * 256].rearrange("p (a t) -> p a t", a=nbk),
                        in1=gbias_bc[:, h * 256:(h + 1) * 256].unsqueeze(1).to_broadcast([128, nbk, 256]), op=ALU.add),
                        reads=[("ps", 6 + b), "gbias"], writes=["gkt_all"])
                kb.op("act", lambda e: e.activation(out=gkt_all[:, :, :], in_=gkt_all[:, :, :], func=AF.Exp, scale=-1.0),
                      reads=["gkt_all"], writes=["gkt_all"])
                kb.op("act", lambda e: e.activation(out=gkt_all[:, :, :], in_=gkt_all[:, :, :], func=AF.Ln, bias=1.0, scale=1.0),
                      reads=["gkt_all"], writes=["gkt_all"])
            A(gate_all)
            fronts, backs = [], []
            for tt in range(NTT):
                c0 = C0 + tt * 128
                pr = tt % 2
                gk_, eb_, enb_, qtl, ktl, ktk, ATb = gkt_all[:, tt, :], ebTs[pr], enbTs[pr], qtls[pr], ktls[pr], ktoks[pr], ATbs[pr]
                KG, KE, KN, KQ, KK, KT, KA = "gkt_all", ("ebT", pr), ("enbT", pr), ("qtl", pr), ("ktl", pr), ("ktok", pr), ("ATb", pr)
                F, Bk = [], []

                def b2(c0=c0, gk_=gk_, eb_=eb_, enb_=enb_, qtl=qtl, ktl=ktl, KG=KG, KE=KE, KN=KN, KQ=KQ, KK=KK):
                    for kc in range(2):
                        kb.op("pe", lambda e, kc=kc: e.matmul(ps[3][:, kc * 128:(kc + 1) * 128],
                                                              lhsT=gk_[:, kc * 128:(kc + 1) * 128], rhs=Umask,
                                                              start=True, stop=True),
                              reads=[KG, "cst"], writes=[("ps", 3)], inc=(kc == 1))
                    kb.op("act", lambda e: e.activation(out=eb_[:, :], in_=ps[3][:, 0:256], func=AF.Exp, scale=-1.0 / 16),
                          reads=[("ps", 3)], writes=[KE])
                    kb.op("act", lambda e: e.activation(out=enb_[:, :], in_=ps[3][:, 0:256], func=AF.Exp, scale=1.0 / 16),
                          reads=[("ps", 3), KE], writes=[KN])
                    kb.op("dve", lambda e: e.tensor_tensor(out=qtl[:, :, :], in0=qT[:, :, c0:c0 + 128],
                                                           in1=eb_.rearrange("p (a t) -> p a t", a=2), op=ALU.mult),
                          reads=[QK, KE], writes=[KQ])
                    kb.op("dve", lambda e: e.tensor_tensor(out=ktl[:, :, :], in0=kT[:, :, c0:c0 + 128],
                                                           in1=enb_.rearrange("p (a t) -> p a t", a=2), op=ALU.mult),
                          reads=[QK, KN], writes=[KK])
                F.append(b2)

                def b3(qtl=qtl, ktl=ktl, ktk=ktk, ATb=ATb, KQ=KQ, KK=KK, KT=KT, KA=KA):
                    for kc in range(2):
                        kb.op("pe", lambda e, kc=kc: e.transpose(out=psb[3][:, 512 + kc * 128:512 + (kc + 1) * 128], in_=ktl[:, kc, :],
                                                                  identity=identb[:]),
                              reads=[KK, "identb"], writes=[("ps", 3)], inc=(kc == 1))
                    kb.op("act", lambda e: e.copy(out=ktk[:, :], in_=psb[3][:, 512:768]), reads=[("ps", 3)], writes=[KT])
                    for kc in range(2):
                        kb.op("pe", lambda e, kc=kc: e.matmul(ps[2][:, 256:384], lhsT=ktl[:, kc, :], rhs=qtl[:, kc, :],
                                                              start=(kc == 0), stop=(kc == 1)),
                              reads=[KK, KQ], writes=[("ps", 2)], inc=(kc == 1))
                    kb.op("dve", lambda e: e.tensor_tensor(out=ATb[:, :], in0=ps[2][:, 256:384], in1=Umask, op=ALU.mult),
                          reads=[("ps", 2), "cst"], writes=[KA])
                F.append(b3)

                def b4(tt=tt, eb_=eb_, qtl=qtl, ktk=ktk, ATb=ATb, KE=KE, KQ=KQ, KT=KT, KA=KA):
                    kb.op("pe", lambda e: e.matmul(ps[5][:, :], lhsT=ATb[:, :], rhs=vb[:, tt, :], start=True, stop=False),
                          reads=[KA, VB], writes=[("ps", 5)], inc=False)
                    for kc in range(2):
                        kb.op("pe", lambda e, kc=kc: e.matmul(ps[5][:, :], lhsT=qtl[:, kc, :], rhs=Sb[:, kc, :],
                                                              start=False, stop=(kc == 1)),
                              reads=[KQ, "Sb"], writes=[("ps", 5)], inc=(kc == 1))
                    for kc in range(2):
                        kb.op("pe", lambda e, kc=kc: e.matmul(ps[6 + kc][:, :], lhsT=ktk[:, kc * 128:(kc + 1) * 128],
                                                              rhs=vb[:, tt, :], start=True, stop=True),
                              reads=[KT, VB], writes=[("ps", 6 + kc)])
                        kb.op("dve", lambda e, kc=kc: e.tensor_tensor(out=Sst[:, kc, :], in0=ps[6 + kc][:, :],
                                                                      in1=Sst[:, kc, :], op=ALU.add),
                              reads=[("ps", 6 + kc), "S"], writes=["S"])
                        kb.op("dve", lambda e, kc=kc: e.tensor_scalar(out=Sst[:, kc, :], in0=Sst[:, kc, :],
                                                                      scalar1=eb_[:, kc * 128 + 127:kc * 128 + 128],
                                                                      scalar2=None, op0=ALU.mult),
                              reads=["S", KE], writes=["S"])
                    kb.op("act", lambda e: e.copy(out=Sb[:, :, :], in_=Sst[:]), reads=["S"], writes=["Sb"])
                Bk.append(b4)
                n0 = len(steps)
                o_epilogue_steps(128, 5, tt, c0)
                Bk.extend(steps[n0:])
                del steps[n0:]
                fronts.append(F)
                backs.append(Bk)
            steps.extend(fronts[0])
            for tt in range(NTT):
                nxt = fronts[tt + 1] if tt + 1 < NTT else []
                bk = backs[tt]
                for i in range(max(len(bk), len(nxt))):
                    if i < len(bk):
                        steps.append(bk[i])
                    if i < len(nxt):
                        steps.append(nxt[i])

            def s_fin():
                kb.op("sp", lambda e: e.dma_start(out=ngp.ap()[j, h].rearrange("(c p) v -> p c v", p=128), in_=Sst[:]),
                      reads=["S"], writes=["ngp"], dma=misc_sem())
            A(s_fin)
            if p == 0:
                tt = NTT

                def c1():
                    kb.op("pe", lambda e: e.matmul(ps[2][0:NS, 0:256], lhsT=gkl[0:16, SC0:SC0 + NS],
                                                   rhs=wgu[0:16, h * 256:(h + 1) * 256], start=True, stop=True),
                          reads=["gkl", "wgu"], writes=[("ps", 2)])
                    kb.op("dve", lambda e: e.tensor_tensor(out=gkt[0:NS, :], in0=ps[2][0:NS, 0:256],
                                                           in1=gbias_bc[0:NS, h * 256:(h + 1) * 256], op=ALU.add),
                          reads=[("ps", 2), "gbias"], writes=["gkt_all"])
                    kb.op("act", lambda e: e.activation(out=gkt[0:NS, :], in_=gkt[0:NS, :], func=AF.Exp, scale=-1.0),
                          reads=["gkt_all"], writes=["gkt_all"])
                    kb.op("act", lambda e: e.activation(out=gkt[0:NS, :], in_=gkt[0:NS, :], func=AF.Ln, bias=1.0, scale=1.0),
                          reads=["gkt_all"], writes=["gkt_all"])
                A(c1)

                def c2():
                    for kc in range(2):
                        kb.op("pe", lambda e, kc=kc: e.matmul(ps[3][:, kc * NS:(kc + 1) * NS],
                                                              lhsT=gkt[0:NS, kc * 128:(kc + 1) * 128], rhs=cst[0:NS, 0:NS],
                                                              start=True, stop=True),
                              reads=["gkt_all", "cst"], writes=[("ps", 3)], inc=(kc == 1))
                    kb.op("act", lambda e: e.activation(out=aT[:, :], in_=ps[3][:, 0:2 * NS], func=AF.Exp, scale=-1.0 / 16),
                          reads=[("ps", 3)], writes=["aT"])
                    kb.op("dve", lambda e: e.tensor_copy(out=kTf[:, :, :], in_=kT[:, :, SC0:SC0 + NS]), reads=[QK], writes=["kTf"])
                A(c2)

                def c3():
                    for kc in range(2):
                        kb.op("pe", lambda e, kc=kc: e.transpose(out=ps[3][0:NS, 256 + kc * 128:256 + (kc + 1) * 128],
                                                                  in_=kTf[:, kc, :], identity=ident),
                              reads=["kTf", "cst"], writes=[("ps", 3)], inc=(kc == 1))
                    kb.op("act", lambda e: e.copy(out=ktok[0:NS, :], in_=ps[3][0:NS, 256:512]), reads=[("ps", 3)], writes=[("ktok", 0)])
                    for kc in range(2):
                        kb.op("dve", lambda e, kc=kc: e.tensor_tensor(
                            out=Qm[:, kc, :, :], in0=qT[:, kc, SC0:SC0 + NS].unsqueeze(2).to_broadcast([128, NS, NS]),
                            in1=eyeb, op=ALU.mult),
                            reads=[QK, "cst"], writes=["Qm"])
                A(c3)
                for s_ in range(NS):
                    def d1(s=s_):
                        b2_ = s % 2
                        kb.op("sp", lambda e: e.dma_start(
                            out=Ss[b2_][:], in_=sgla.ap()[j, s, h].rearrange("(c p) v -> p c v", p=128)),
                            writes=[("Ss", b2_)], dma="ss%d" % b2_)
                        kb.op("dve", lambda e: e.tensor_scalar(out=Km[b2_][0:NS, :], in0=ktok[0:NS, :],
                                                               scalar1=cst[0:NS, s:s + 1], scalar2=None, op0=ALU.mult),
                              reads=[("ktok", 0), "cst"], writes=[("Km", b2_)])
                        for kc in range(2):
                            kb.op("pe", lambda e, kc=kc: e.matmul(ps[6 + kc][:, :], lhsT=Km[b2_][0:NS, kc * 128:(kc + 1) * 128],
                                                                  rhs=vb[0:NS, tt, :], start=True, stop=True),
                                  reads=[("Km", b2_), VB], writes=[("ps", 6 + kc)])
                            kb.op("dve", lambda e, kc=kc: e.scalar_tensor_tensor(
                                out=Sn[b2_][:, kc, :], in0=Ss[b2_][:, kc, :], scalar=aT[:, kc * NS + s:kc * NS + s + 1],
                                in1=ps[6 + kc][:, :], op0=ALU.mult, op1=ALU.add),
                                reads=[("Ss", b2_), "aT", ("ps", 6 + kc)], writes=[("Sn", b2_)])
                        kb.op("sp", lambda e: e.dma_start(
                            out=ngs.ap()[j, s, h].rearrange("(c p) v -> p c v", p=128), in_=Sn[b2_][:]),
                            reads=[("Sn", b2_)], dma="sn%d" % b2_)
                        kb.op("act", lambda e: e.copy(out=Snb[b2_][:, :, :], in_=Sn[b2_][:]),
                              reads=[("Sn", b2_)], writes=["Snb"])
                    A(d1)

                    def d2(s=s_):
                        b2_ = s % 2
                        for kc in range(2):
                            kb.op("pe", lambda e, kc=kc: e.matmul(
                                ps[4][0:NS, :], lhsT=Qm[:, kc, s, :], rhs=Snb[b2_][:, kc, :],
                                start=(s == 0 and kc == 0), stop=(s == NS - 1 and kc == 1)),
                                reads=["Qm", "Snb"], writes=[("ps", 4)], inc=(kc == 1))
                    A(d2)
                o_epilogue_steps(NS, 4, tt, SC0)
            return steps

        def wout(h):
            sB = load_WB(w_out.ap()[j, h * 512:(h + 1) * 512, :].rearrange("(c p) d -> p c d", p=128))
            emit_accum_mm(gs, sB, onT, "onT", 4)

        for st_ in proj_steps(0):
            st_()
        for h in range(4):
            cs = chain_steps(h)
            psn = proj_steps(h + 1) if h < 3 else []
            ci = 0
            for pi, pstep in enumerate(psn):
                pstep()
                want = (len(cs) * (pi + 1)) // max(len(psn), 1)
                while ci < want:
                    cs[ci]()
                    ci += 1
            while ci < len(cs):
                cs[ci]()
                ci += 1
            wout(h)
        kb.op("dve", lambda e: e.memset(ssq[:, :], 0.0), reads=[],
              writes=["ssq", ("stage", 0), ("stage", 1), ("vb", 1), ("sog", 1)])

    for p in range(NPASS):
        emit_in(p)
        for l in range(NL):
            if l % 2 == 0:
                if MIX in ("pool", "all"):
                    emit_pool(p, l)
            else:
                if MIX == "all":
                    emit_gla(p, l)
            if cfg.get("do_ln", True):
                emit_ln(p, ROW_GM + l * 16, ROW_BM + l * 16, barrier=not (l % 2 == 0 and p > 0 and MIX != "none"))
            if cfg.get("do_mlp", True):
                emit_mlp(p, l)
            if cfg.get("do_ln", True):
                emit_ln(p, ROW_GF + l * 16, ROW_BF + l * 16, barrier=False)
        emit_out(p)

    kb.emit()
    return nc


def make_consts():
    c = np.zeros((128, CW), np.float32)
    c[:, 0:128] = np.eye(128, dtype=np.float32)
    j = np.arange(128)[:, None]
    i = np.arange(128)[None, :]
    c[:, 128:256] = (j <= i).astype(np.float32)
    for g in range(4):
        w = 2 ** (g + 1)
        t = np.arange(16)
        c[:, 256 + g * 16:256 + (g + 1) * 16] = (1.0 / np.minimum(t + 1, w))[None, :]
    c[:, 320:576] = np.eye(16, dtype=np.float32).reshape(1, 256)
    return c


def make_vecs(inp):
    rows = [
        np.asarray(inp["mlp_b1"], np.float32).reshape(-1, 128),
        np.asarray(inp["mlp_b2"], np.float32).reshape(-1, 128),
        np.asarray(inp["ln_mix_g"], np.float32).reshape(-1, 128),
        np.asarray(inp["ln_mix_b"], np.float32).reshape(-1, 128),
        np.asarray(inp["ln_ffn_g"], np.float32).reshape(-1, 128),
        np.asarray(inp["ln_ffn_b"], np.float32).reshape(-1, 128),
        np.asarray(inp["pool_scale"], np.float32).reshape(-1, 128),
    ]
    v = np.concatenate(rows, axis=0)
    out = np.zeros((NROWS_PAD, 128), np.float32)
    out[: v.shape[0]] = v
    return out


N_CORES = 8
PROMPT_CORES = (0, 1, 4, 5)
TP_RUN = 512
_NC_CACHE = {}


def kernel(x_prompt, x_sample, state_pool, state_gla, pool_w, pool_scale, gla_w_in, gla_w_gate_up,
           gla_gate_bias, gla_norm_w, gla_w_out, ln_mix_g, ln_mix_b, mlp_w1, mlp_b1, mlp_w2, mlp_b2,
           ln_ffn_g, ln_ffn_b):
    f = lambda a: np.ascontiguousarray(np.asarray(a), dtype=np.float32)
    x_prompt = f(x_prompt)
    x_sample = f(x_sample)
    state_pool = f(state_pool)
    state_gla = f(state_gla)
    B, L, _ = x_prompt.shape
    npass = L // TP_RUN
    key = (npass, TP_RUN)
    if key not in _NC_CACHE:
        _NC_CACHE[key] = build(dict(npass=npass, tp=TP_RUN, layers=DEPTH, mixers="all"))
    nc = _NC_CACHE[key]
    vec_in = dict(mlp_b1=mlp_b1, mlp_b2=mlp_b2, ln_mix_g=ln_mix_g, ln_mix_b=ln_mix_b, ln_ffn_g=ln_ffn_g,
                  ln_ffn_b=ln_ffn_b, pool_scale=pool_scale)
    shared = dict(vecs=make_vecs(vec_in), consts=make_consts(), mlp_w1=f(mlp_w1), mlp_w2=f(mlp_w2),
                  pool_w=f(pool_w), gla_w_in=f(gla_w_in), gla_wgu=f(gla_w_gate_up), gla_gbias=f(gla_gate_bias),
                  gla_normw=f(gla_norm_w), gla_w_out=f(gla_w_out))
    in_maps = []
    real = {PROMPT_CORES[b]: b for b in range(B)}
    zero_prompt = np.zeros_like(x_prompt[0])
    for c in range(N_CORES):
        m = dict(shared)
        m["xp"] = x_prompt[real[c]] if c in real else zero_prompt
        m["xs"] = x_sample[c * NS:(c + 1) * NS, 0]
        m["spool"] = np.ascontiguousarray(state_pool[:, c * NS:(c + 1) * NS])
        m["sgla"] = np.ascontiguousarray(state_gla[:, c * NS:(c + 1) * NS])
        in_maps.append(m)
    res = run_bass_kernel_spmd(nc, in_maps, core_ids=list(range(N_CORES)))
    r = res.results
    pc = PROMPT_CORES
    y_prompt = np.stack([r[pc[b]]["yp"] for b in range(B)], axis=0)
    y_sample = np.concatenate([r[c]["ys"] for c in range(N_CORES)], axis=0)[:, None, :]
    new_pool_prompt = np.stack([r[pc[b]]["npp"] for b in range(B)], axis=1)
    new_gla_prompt = np.stack([r[pc[b]]["ngp"] for b in range(B)], axis=1)
    new_pool_sample = np.concatenate([r[c]["nps"] for c in range(N_CORES)], axis=1)
    new_gla_sample = np.concatenate([r[c]["ngs"] for c in range(N_CORES)], axis=1)
    return (y_prompt.astype(np.float32), y_sample.astype(np.float32), new_pool_prompt.astype(np.float32),
            new_gla_prompt.astype(np.float32), new_pool_sample.astype(np.float32), new_gla_sample.astype(np.float32))
```

```python
import math
from contextlib import ExitStack

import numpy as np
import concourse.bass as bass
import concourse.mybir as mybir
from concourse.bass_utils import run_bass_kernel_spmd

F32 = mybir.dt.float32
BF16 = mybir.dt.bfloat16
AF = mybir.ActivationFunctionType
ALU = mybir.AluOpType
AX = mybir.AxisListType

D = 2048
DC = 16
DFF = 8192
FC = 64
DEPTH = 4
NS = 16
HALO = 16
ALPHA = (2 * DEPTH) ** 0.25
INV_ALPHA = 1.0 / ALPHA
LN_EPS = 1e-5
EPS_P = LN_EPS / (ALPHA * ALPHA)
GLA_IN = 6160
G_MLP = 4

ROW_B1 = 0
ROW_B2 = ROW_B1 + 4 * 64
ROW_GM = ROW_B2 + 4 * 16
ROW_BM = ROW_GM + 4 * 16
ROW_GF = ROW_BM + 4 * 16
ROW_BF = ROW_GF + 4 * 16
ROW_PS = ROW_BF + 4 * 16
NROWS = ROW_PS + 2 * 16
NROWS_PAD = 640


class KB:
    ENG = ("pe", "act", "dve", "pool", "sp")

    def __init__(self, nc):
        self.nc = nc
        self.ops = {e: [] for e in self.ENG}
        self.cnt = {}
        self.seen = {e: {} for e in self.ENG}
        self.trk = {}

    def op(self, eng, fn, reads=(), writes=(), inc=True, dma=None):
        deps = {}

        def add(d):
            for k, v in d.items():
                if deps.get(k, 0) < v:
                    deps[k] = v

        for key in reads:
            t = self.trk.get(key)
            if t:
                add(t["w"])
        for key in writes:
            t = self.trk.get(key)
            if t:
                add(t["w"])
                add(t["r"])
        waits = []
        for k, v in deps.items():
            if k == "pe" and eng == "pe":
                continue
            if self.seen[eng].get(k, 0) < v:
                self.seen[eng][k] = v
                waits.append((k, v))
        if dma is not None:
            prev = self.cnt.get(dma, 0)
            if prev and self.seen[eng].get(dma, 0) < prev:
                self.seen[eng][dma] = prev
                waits.append((dma, prev))
            self.cnt[dma] = prev + 16
            my = {dma: prev + 16}
            incr = (dma, 16)
        else:
            c = self.cnt.get(eng, 0)
            if inc:
                self.cnt[eng] = c + 1
                incr = (eng, 1)
            else:
                incr = None
            my = {eng: c + 1}
        for key in reads:
            t = self.trk.setdefault(key, {"w": {}, "r": {}})
            for k, v in my.items():
                if t["r"].get(k, 0) < v:
                    t["r"][k] = v
        for key in writes:
            self.trk[key] = {"w": dict(my), "r": {}}
        self.ops[eng].append((waits, fn, incr))

    def barrier(self, engs=("pe", "act", "dve")):
        for e in engs:
            waits = []
            for k in engs:
                v = self.cnt.get(k, 0)
                if k != e and v and self.seen[e].get(k, 0) < v:
                    self.seen[e][k] = v
                    waits.append((k, v))
            if waits:
                self.ops[e].append((waits, None, None))

    def emit(self, final_wait_eng="sp"):
        nc = self.nc
        keys = list(self.cnt.keys())
        with ExitStack() as es:
            sems = {}
            for i, k in enumerate(keys):
                sems[k] = es.enter_context(nc.semaphore("s%d" % i))
            fw = [(k, v) for k, v in self.cnt.items() if k not in self.ENG]
            fw += [(k, self.cnt[k]) for k in ("pe", "act", "dve") if k in self.cnt]
            self.ops[final_wait_eng].append((fw, None, None))

            def replay(name, e):
                for waits, fn, incr in self.ops[name]:
                    for k, v in waits:
                        e.wait_ge(sems[k], v)
                    if fn is None:
                        continue
                    ins = fn(e)
                    if incr is not None:
                        ins.then_inc(sems[incr[0]], incr[1])

            with nc.Block() as block:
                @block.tensor
                def _(e):
                    replay("pe", e)

                @block.scalar
                def _(e):
                    replay("act", e)

                @block.vector
                def _(e):
                    replay("dve", e)

                @block.gpsimd
                def _(e):
                    replay("pool", e)

                @block.sync
                def _(e):
                    replay("sp", e)


class Arena:
    def __init__(self, h, nwords):
        self.h = h
        self.hb = h.bitcast(BF16)
        self.n = nwords
        self.off = 0

    def reset(self):
        self.off = 0

    def _take(self, words):
        o = self.off
        self.off += words
        assert self.off <= self.n, ("arena overflow", self.off, self.n)
        return o

    @staticmethod
    def _shape(ap, shape):
        if len(shape) == 2:
            return ap
        if len(shape) == 3:
            return ap.rearrange("p (a b) -> p a b", a=shape[1])
        if len(shape) == 4:
            return ap.rearrange("p (a b c) -> p a b c", a=shape[1], b=shape[2])
        raise ValueError(shape)

    def f32(self, shape):
        n = int(np.prod(shape[1:]))
        o = self._take(n)
        return self._shape(self.h[:, o:o + n], shape)

    def bf16(self, shape):
        n = int(np.prod(shape[1:]))
        o = self._take((n + 1) // 2)
        return self._shape(self.hb[:, 2 * o:2 * o + n], shape)


CW = 576


def build(cfg):
    NPASS = cfg["npass"]
    TP = cfg["tp"]
    NL = cfg.get("layers", DEPTH)
    MIX = cfg.get("mixers", "all")
    NTOK = NPASS * TP
    COLS = HALO + TP + NS
    C0 = HALO
    SC0 = C0 + TP
    NTT = TP // 128

    nc = bass.Bass("TRN2", target_bir_lowering=False)
    kb = KB(nc)

    def din(name, shape):
        return nc.dram_tensor(name, list(shape), F32, kind="ExternalInput")

    def dout(name, shape):
        return nc.dram_tensor(name, list(shape), F32, kind="ExternalOutput")

    xp = din("xp", [NTOK, D])
    xs = din("xs", [NS, D])
    vecs = din("vecs", [NROWS_PAD, 128])
    consts = din("consts", [128, CW])
    w1 = din("mlp_w1", [DEPTH, D, DFF])
    w2 = din("mlp_w2", [DEPTH, DFF, D])
    yp = dout("yp", [NTOK, D])
    ys = dout("ys", [NS, D])
    if MIX != "none":
        spool = din("spool", [2, NS, 15, D])
        pool_w = din("pool_w", [2, 4, 512, 512])
        npp = dout("npp", [2, 15, D])
        nps = dout("nps", [2, NS, 15, D])
    if MIX == "all":
        sgla = din("sgla", [2, NS, 4, 256, 512])
        w_in = din("gla_w_in", [2, D, GLA_IN])
        d_wgu = din("gla_wgu", [2, 16, 1024])
        d_gbias = din("gla_gbias", [2, 1024])
        d_normw = din("gla_normw", [2, 512])
        w_out = din("gla_w_out", [2, D, D])
        ngp = dout("ngp", [2, 4, 256, 512])
        ngs = dout("ngs", [2, NS, 4, 256, 512])

    sb = lambda name, shape, dt=F32: nc.alloc_sbuf_tensor(name, list(shape), dt)
    xT = sb("xT", [128, DC, COLS])
    xTb = sb("xTb", [128, DC, COLS], BF16)
    cols = sb("cols", [128, NROWS_PAD])
    cst = sb("cst", [128, CW])
    identb = sb("identb", [128, 128], BF16)
    onesb = sb("onesb", [128, 128], BF16)
    epsc = sb("epsc", [128, 2])
    fillb = sb("fillb", [128, 512], BF16)
    stage = [sb("stage%d" % i, [128, D]) for i in range(2)]
    WA = [sb("WA%d" % i, [128, 4096], BF16) for i in range(3)]
    WB = [sb("WB%d" % i, [128, G_MLP, D], BF16) for i in range(2)]
    halo_save = [sb("halo%d" % i, [128, DC, 16]) for i in range(2)]
    if MIX == "all":
        wgu = sb("wgu", [16, 1024])
        gbias_bc = sb("gbias_bc", [128, 1024])
        normw_bc = sb("normw_bc", [128, 512])
        Ss = [sb("Ss%d" % i, [128, 2, 512]) for i in range(2)]
        Sn = [sb("Sn%d" % i, [128, 2, 512]) for i in range(2)]
        Sst = sb("Sst", [128, 2, 512])
    rem = int(nc.sbuf_bytes_remaining) - 256
    AW = rem // 4
    ar = Arena(sb("arena", [128, AW]), AW)

    ps = [nc.alloc_psum_tensor("ps%d" % i, [128, 512], F32) for i in range(8)]
    psb = [t.bitcast(BF16) for t in ps]
    ident = cst[:, 0:128]
    Umask = cst[:, 128:256]
    invc = cst[:, 256:320]
    eyeb = cst[:, 320:576].rearrange("p (a b) -> p a b", a=16)

    misc_i = [0]

    def misc_sem():
        misc_i[0] += 1
        return "m%d" % (misc_i[0] % 6)

    def wa_view(s, c, n):
        return WA[s][:, 0:c * n].rearrange("p (c n) -> p c n", c=c)

    kb.op("sp", lambda e: e.dma_start(out=cst[:], in_=consts.ap()), writes=["cst"], dma=misc_sem())
    kb.op("dve", lambda e: e.tensor_copy(out=identb[:], in_=cst[:, 0:128]), reads=["cst"], writes=["identb"])
    kb.op("dve", lambda e: e.memset(onesb[:], 1.0 / D), writes=["onesb"])
    kb.op("dve", lambda e: e.memset(fillb[:], 0.0), writes=["fillb"])
    kb.op("dve", lambda e: e.memset(epsc[:, 0:1], EPS_P), writes=["epsc"])
    kb.op("dve", lambda e: e.memset(epsc[:, 1:2], 1e-5), writes=["epsc"])
    for t in range(NROWS_PAD // 128):
        st = stage[t % 2]
        kb.op("sp", lambda e, st=st, t=t: e.dma_start(out=st[:, 0:128], in_=vecs.ap()[t * 128:(t + 1) * 128, :]),
              writes=[("stage", t % 2)], dma=misc_sem())
        kb.op("pe", lambda e, st=st: e.transpose(out=ps[0][:, 0:128], in_=st[:, 0:128], identity=ident),
              reads=[("stage", t % 2), "cst"], writes=[("ps", 0)])
        kb.op("act", lambda e, t=t: e.copy(out=cols[:, t * 128:(t + 1) * 128], in_=ps[0][:, 0:128]),
              reads=[("ps", 0)], writes=["cols"])
    kb.op("dve", lambda e: e.tensor_scalar(out=cols[:, ROW_B2:ROW_B2 + 64], in0=cols[:, ROW_B2:ROW_B2 + 64],
                                           scalar1=INV_ALPHA, scalar2=None, op0=ALU.mult),
          reads=["cols"], writes=["cols"])
    kb.op("dve", lambda e: e.tensor_scalar(out=cols[:, ROW_PS:ROW_PS + 32], in0=cols[:, ROW_PS:ROW_PS + 32],
                                           scalar1=INV_ALPHA, scalar2=None, op0=ALU.mult),
          reads=["cols"], writes=["cols"])

    wa_i = [0]
    wb_i = [0]

    def load_WA(src_ap, c, n):
        s = wa_i[0] % 3
        wa_i[0] += 1
        dst = wa_view(s, c, n)
        kb.op("pool", lambda e: e.dma_start(out=dst, in_=src_ap), writes=[("WA", s)], dma="wa%d" % s)
        return s

    def load_WB(src_ap):
        s = wb_i[0] % 2
        wb_i[0] += 1
        kb.op("pool", lambda e: e.dma_start(out=WB[s][:], in_=src_ap), writes=[("WB", s)], dma="wb%d" % s)
        return s

    def xkeys():
        return [("xT", i) for i in range(DC)]

    def xbkeys():
        return [("xTb", i) for i in range(DC)]

    def groups(p):
        g = [(C0 + i * 512, min(512, TP - i * 512)) for i in range((TP + 511) // 512)]
        if p == 0:
            g.append((SC0, NS))
        return g

    def emit_in(p):
        for i in range(NTT):
            st = stage[i % 2]
            r0 = p * TP + i * 128
            kb.op("sp", lambda e, st=st, r0=r0: e.dma_start(out=st[:], in_=xp.ap()[r0:r0 + 128, :]),
                  writes=[("stage", i % 2)], dma=misc_sem())
            for q in range(4):
                b = q % 2
                for cc in range(4):
                    c = q * 4 + cc
                    kb.op("pe", lambda e, st=st, c=c, cc=cc, b=b: e.transpose(
                        out=ps[b][:, cc * 128:(cc + 1) * 128], in_=st[:, c * 128:(c + 1) * 128], identity=ident),
                        reads=[("stage", i % 2), "cst"], writes=[("ps", b)], inc=(cc == 3))
                col = C0 + i * 128
                kb.op("act", lambda e, q=q, b=b, col=col: e.copy(
                    out=xT[:, q * 4:(q + 1) * 4, col:col + 128],
                    in_=ps[b][:].rearrange("p (a t) -> p a t", a=4)),
                    reads=[("ps", b)], writes=[("xT", q * 4 + k) for k in range(4)])
                kb.op("dve", lambda e, q=q, col=col: e.tensor_copy(
                    out=xTb[:, q * 4:(q + 1) * 4, col:col + 128],
                    in_=xT[:, q * 4:(q + 1) * 4, col:col + 128]),
                    reads=[("xT", q * 4 + k) for k in range(4)], writes=[("xTb", q * 4 + k) for k in range(4)])
        if p == 0:
            st = stage[0]
            kb.op("sp", lambda e: e.dma_start(out=st[0:NS, :], in_=xs.ap()), writes=[("stage", 0)], dma=misc_sem())
            for c in range(DC):
                kb.op("pe", lambda e, c=c: e.transpose(out=ps[2][:, c * NS:(c + 1) * NS],
                                                        in_=st[0:NS, c * 128:(c + 1) * 128],
                                                        identity=cst[0:NS, 0:NS]),
                      reads=[("stage", 0), "cst"], writes=[("ps", 2)], inc=(c == DC - 1))
            kb.op("act", lambda e: e.copy(out=xT[:, :, SC0:SC0 + NS],
                                          in_=ps[2][:, 0:DC * NS].rearrange("p (c s) -> p c s", c=DC)),
                  reads=[("ps", 2)], writes=xkeys())
            kb.op("dve", lambda e: e.tensor_copy(out=xTb[:, :, SC0:SC0 + NS], in_=xT[:, :, SC0:SC0 + NS]),
                  reads=xkeys(), writes=xbkeys())

    def emit_rows_out(src_of_chunk, M, dst_ap, sidx):
        st = stage[sidx]
        for q in range(4):
            for cc in range(4):
                c = q * 4 + cc
                kb.op("pe", lambda e, c=c, cc=cc: e.transpose(
                    out=ps[3][0:M, cc * 128:(cc + 1) * 128], in_=src_of_chunk(c), identity=ident),
                    reads=xkeys() + ["cst", "halo"], writes=[("ps", 3)], inc=(cc == 3))
            kb.op("act", lambda e, q=q: e.copy(out=st[0:M, q * 512:(q + 1) * 512], in_=ps[3][0:M, :]),
                  reads=[("ps", 3)], writes=[("stage", sidx)])
        kb.op("sp", lambda e: e.dma_start(out=dst_ap, in_=st[0:M, :]), reads=[("stage", sidx)], dma=misc_sem())

    def emit_out(p):
        for i in range(NTT):
            col = C0 + i * 128
            r0 = p * TP + i * 128
            emit_rows_out(lambda c, col=col: xT[:, c, col:col + 128], 128, yp.ap()[r0:r0 + 128, :], i % 2)
        if p == 0:
            emit_rows_out(lambda c: xT[:, c, SC0:SC0 + NS], NS, ys.ap(), 0)

    MLP_WORDS = 2 * ((G_MLP * COLS + 1) // 2) + 2 * (512 + NS)

    def emit_fill(n):
        for i in range(n):
            kb.op("pe", lambda e: e.matmul(ps[7][:, :], lhsT=fillb[:, 0:128], rhs=fillb[:, :], start=True, stop=True),
                  reads=["fillb"], writes=[("ps", 7)], inc=(i == n - 1))

    def emit_ln(p, grow, brow, barrier=True):
        if barrier:
            kb.barrier()
        ar.off = MLP_WORDS
        ln_mean = ar.f32([128, COLS])
        ln_rstd = ar.f32([128, COLS])
        ln_nmr = ar.f32([128, COLS])
        ln_zb = [ar.bf16([128, 4, COLS]) for _ in range(2)]
        ln_zq = [ar.bf16([128, 4, COLS]) for _ in range(2)]
        ln_u1 = ar.f32([128, 4, COLS])
        ln_u = [ln_u1, ln_u1]
        gs = groups(p)
        lo = C0
        hi = gs[-1][0] + gs[-1][1]
        n = hi - lo
        for q in range(4):
            b = q % 2
            xk = [("xT", q * 4 + k) for k in range(4)]
            kb.op("dve", lambda e, q=q, b=b: e.tensor_copy(out=ln_zb[b][:, :, lo:hi], in_=xT[:, q * 4:q * 4 + 4, lo:hi]),
                  reads=xk, writes=[("zb", b)])
            kb.op("act", lambda e, q=q, b=b: e.activation(out=ln_zq[b][:, :, lo:hi], in_=xT[:, q * 4:q * 4 + 4, lo:hi], func=AF.Square),
                  reads=xk, writes=[("zq", b)])
            for cc in range(4):
                c = q * 4 + cc
                for gi, (g0, gn) in enumerate(gs):
                    kb.op("pe", lambda e, b=b, cc=cc, g0=g0, gn=gn, gi=gi, c=c: e.matmul(
                        ps[gi][:, 0:gn], lhsT=onesb[:], rhs=ln_zb[b][:, cc, g0:g0 + gn], start=(c == 0), stop=(c == DC - 1)),
                        reads=[("zb", b), "onesb"], writes=[("ps", gi)], inc=False)
                    kb.op("pe", lambda e, b=b, cc=cc, g0=g0, gn=gn, gi=gi, c=c: e.matmul(
                        ps[4 + gi][:, 0:gn], lhsT=onesb[:], rhs=ln_zq[b][:, cc, g0:g0 + gn], start=(c == 0), stop=(c == DC - 1)),
                        reads=[("zq", b), "onesb"], writes=[("ps", 4 + gi)], inc=(cc == 3 and gi == len(gs) - 1))
        for gi, (g0, gn) in enumerate(gs):
            sl = slice(g0, g0 + gn)
            kb.op("act", lambda e, gi=gi, gn=gn, sl=sl: e.copy(out=ln_mean[:, sl], in_=ps[gi][:, 0:gn]),
                  reads=[("ps", gi)], writes=["ln_mean"])
            kb.op("dve", lambda e, sl=sl: e.tensor_tensor(out=ln_nmr[:, sl], in0=ln_mean[:, sl], in1=ln_mean[:, sl], op=ALU.mult),
                  reads=["ln_mean"], writes=["ln_nmr"])
            kb.op("dve", lambda e, gi=gi, gn=gn, sl=sl: e.tensor_tensor(out=ln_rstd[:, sl], in0=ps[4 + gi][:, 0:gn], in1=ln_nmr[:, sl], op=ALU.subtract),
                  reads=[("ps", 4 + gi), "ln_nmr"], writes=["ln_rstd"])
            kb.op("act", lambda e, sl=sl: e.activation(out=ln_rstd[:, sl], in_=ln_rstd[:, sl], func=AF.Sqrt, bias=epsc[:, 0:1], scale=1.0),
                  reads=["ln_rstd", "epsc"], writes=["ln_rstd"])
            kb.op("dve", lambda e, sl=sl: e.reciprocal(out=ln_rstd[:, sl], in_=ln_rstd[:, sl]),
                  reads=["ln_rstd"], writes=["ln_rstd"])
            kb.op("dve", lambda e, sl=sl: e.scalar_tensor_tensor(out=ln_nmr[:, sl], in0=ln_mean[:, sl], scalar=-1.0, in1=ln_rstd[:, sl], op0=ALU.mult, op1=ALU.mult),
                  reads=["ln_mean", "ln_rstd"], writes=["ln_nmr"])
        emit_fill(cfg.get("ln_fill", 0))
        rb = ln_rstd[:, lo:hi].unsqueeze(1).to_broadcast([128, 4, n])
        nb = ln_nmr[:, lo:hi].unsqueeze(1).to_broadcast([128, 4, n])
        for q in range(4):
            b = q % 2
            xk = [("xT", q * 4 + k) for k in range(4)]
            kb.op("dve", lambda e, q=q, b=b: e.tensor_tensor(out=ln_u[b][:, :, lo:hi], in0=xT[:, q * 4:q * 4 + 4, lo:hi], in1=rb, op=ALU.mult),
                  reads=xk + ["ln_rstd"], writes=["lnu"])
            kb.op("dve", lambda e, b=b: e.tensor_tensor(out=ln_u[b][:, :, lo:hi], in0=ln_u[b][:, :, lo:hi], in1=nb, op=ALU.add),
                  reads=["lnu", "ln_nmr"], writes=["lnu"])
            for cc in range(4):
                c = q * 4 + cc
                kb.op("act", lambda e, c=c, cc=cc, b=b: e.activation(out=xT[:, c, lo:hi], in_=ln_u[b][:, cc, lo:hi], func=AF.Identity,
                                                              bias=cols[:, brow + c:brow + c + 1], scale=cols[:, grow + c:grow + c + 1]),
                      reads=["lnu", "cols"], writes=[("xT", c)])
                kb.op("act", lambda e, c=c, cc=cc, b=b: e.activation(out=xTb[:, c, lo:hi], in_=ln_u[b][:, cc, lo:hi], func=AF.Identity,
                                                              bias=cols[:, brow + c:brow + c + 1], scale=cols[:, grow + c:grow + c + 1]),
                      reads=["lnu", "cols"], writes=[("xTb", c)])

    ybank = [0]

    def emit_accum_mm(gs, sB, src, skey, nk):
        for m in range(DC):
            for gi, (g0, gn) in enumerate(gs):
                bank = (4 + 2 * gi + (ybank[0] % 2)) if len(gs) > 1 else (4 + ybank[0] % 4)
                for k in range(nk):
                    kb.op("pe", lambda e, k=k, m=m, g0=g0, gn=gn, bank=bank: e.matmul(
                        ps[bank][:, 0:gn], lhsT=WB[sB][:, k, m * 128:(m + 1) * 128],
                        rhs=src[:, k, g0:g0 + gn], start=(k == 0), stop=(k == nk - 1)),
                        reads=[("WB", sB), (skey, k)], writes=[("ps", bank)], inc=(k == nk - 1))
                kb.op("dve", lambda e, m=m, g0=g0, gn=gn, bank=bank: e.scalar_tensor_tensor(
                    out=xT[:, m, g0:g0 + gn], in0=ps[bank][:, 0:gn], scalar=INV_ALPHA,
                    in1=xT[:, m, g0:g0 + gn], op0=ALU.mult, op1=ALU.add),
                    reads=[("ps", bank), ("xT", m)], writes=[("xT", m)])
            ybank[0] += 1

    def emit_mlp(p, l):
        ar.reset()
        hTs = [ar.bf16([128, G_MLP, COLS]) for _ in range(2)]
        rT = [ar.f32([128, 512 + NS]) for _ in range(2)]
        gs = groups(p)
        lo = C0
        hi = gs[-1][0] + gs[-1][1]
        for c in range(DC):
            kb.op("dve", lambda e, c=c: e.tensor_scalar(out=xT[:, c, lo:hi], in0=xT[:, c, lo:hi],
                                                        scalar1=cols[:, ROW_B2 + l * 16 + c:ROW_B2 + l * 16 + c + 1],
                                                        scalar2=None, op0=ALU.add),
                  reads=[("xT", c), "cols"], writes=[("xT", c)])
        st = {}

        def mm1(g):
            hT = hTs[g % 2]
            for jj in range(G_MLP):
                j = g * G_MLP + jj
                if j % 2 == 0:
                    f0 = j * 128
                    st["sA"] = load_WA(w1.ap()[l, :, f0:f0 + 256].rearrange("(c p) f -> p c f", p=128), DC, 256)
                sA = st["sA"]
                Wv = wa_view(sA, DC, 256)
                for gi, (g0, gn) in enumerate(gs):
                    bank = ((2 * gi) + (j % 2)) if len(gs) > 1 else (j % 4)
                    for dc in range(DC):
                        kb.op("pe", lambda e, Wv=Wv, j=j, dc=dc, g0=g0, gn=gn, bank=bank: e.matmul(
                            ps[bank][:, 0:gn], lhsT=Wv[:, dc, (j % 2) * 128:(j % 2) * 128 + 128],
                            rhs=xTb[:, dc, g0:g0 + gn], start=(dc == 0), stop=(dc == DC - 1)),
                            reads=[("WA", sA), ("xTb", dc)], writes=[("ps", bank)], inc=(dc == DC - 1))
                    rb = rT[j % 2]
                    roff = 0 if gi == 0 else 512
                    bcol = ROW_B1 + l * 64 + j
                    kb.op("act", lambda e, bank=bank, gn=gn, rb=rb, roff=roff, bcol=bcol: e.activation(
                        out=rb[:, roff:roff + gn], in_=ps[bank][:, 0:gn], func=AF.Relu,
                        bias=cols[:, bcol:bcol + 1], scale=1.0),
                        reads=[("ps", bank), "cols"], writes=[("rT", j % 2, gi)])
                    kb.op("act", lambda e, rb=rb, roff=roff, gn=gn, jj=jj, g0=g0, hT=hT: e.activation(
                        out=hT[:, jj, g0:g0 + gn], in_=rb[:, roff:roff + gn], func=AF.Square),
                        reads=[("rT", j % 2, gi)], writes=[(("hT", g % 2), jj)])

        def mm2(g):
            r0 = g * G_MLP * 128
            sB = load_WB(w2.ap()[l, r0:r0 + G_MLP * 128, :].rearrange("(j p) d -> p j d", p=128))
            emit_accum_mm(gs, sB, hTs[g % 2], ("hT", g % 2), G_MLP)

        NG = FC // G_MLP
        for g in range(NG):
            mm1(g)
            if g > 0:
                mm2(g - 1)
        mm2(NG - 1)

    def emit_pool(p, l):
        j = l // 2
        kb.barrier()
        ar.reset()
        hi = C0 + TP
        scr = [ar.f32([128, hi]) for _ in range(2)]
        small = ar.f32([128, 16])
        if p == 0:
            stT = ar.f32([128, DC, NS, 16])
            wsum = ar.f32([128, 4, NS])
        gs = groups(p)
        if p == 0:
            kb.op("dve", lambda e: e.memset(xT[:, :, 0:16], 0.0), writes=xkeys())
        else:
            kb.op("dve", lambda e: e.tensor_copy(out=xT[:, :, 1:16], in_=halo_save[j][:, :, 1:16]),
                  reads=["halo"], writes=xkeys())
        kb.op("dve", lambda e: e.tensor_copy(out=halo_save[j][:, :, 1:16], in_=xT[:, :, hi - 15:hi]),
              reads=xkeys(), writes=["halo"])
        if p == NPASS - 1:
            emit_rows_out(lambda c: halo_save[j][:, c, 1:16], 15, npp.ap()[j], 0)
        if p == 0:
            for t in range(2):
                st = stage[t]
                kb.op("sp", lambda e, st=st, t=t: e.dma_start(
                    out=st[0:120, :], in_=spool.ap()[j, t * 8:(t + 1) * 8].rearrange("s r d -> (s r) d")),
                    writes=[("stage", t)], dma=misc_sem())
                for q in range(4):
                    b = q % 2
                    for cc in range(4):
                        c = q * 4 + cc
                        kb.op("pe", lambda e, st=st, c=c, cc=cc, b=b: e.transpose(
                            out=ps[b][:, cc * 120:(cc + 1) * 120], in_=st[0:120, c * 128:(c + 1) * 128],
                            identity=cst[0:120, 0:120]),
                            reads=[("stage", t), "cst"], writes=[("ps", b)], inc=(cc == 3))
                    for cc in range(4):
                        kb.op("act", lambda e, q=q, cc=cc, b=b, t=t: e.copy(
                            out=stT[:, q * 4 + cc, t * 8:(t + 1) * 8, 0:15],
                            in_=ps[b][:, cc * 120:(cc + 1) * 120].rearrange("p (s r) -> p s r", s=8)),
                            reads=[("ps", b)], writes=["stT"])
            kb.op("dve", lambda e: e.tensor_copy(out=stT[:, :, :, 15], in_=xT[:, :, SC0:SC0 + NS]),
                  reads=xkeys() + ["stT"], writes=["stT"])
            kb.op("sp", lambda e: e.dma_start(out=nps.ap()[j, :, 0:14, :].rearrange("s r d -> s (r d)"),
                                              in_=spool.ap()[j, :, 1:15, :].rearrange("s r d -> s (r d)")),
                  dma=misc_sem())
            emit_rows_out(lambda c: xT[:, c, SC0:SC0 + NS], NS, nps.ap()[j, :, 14, :], 1)
        for c in range(DC):
            g = c // 4
            w = 2 ** (g + 1)
            cur = xT[:, c, :]
            ckey = [("xT", c)]
            for s in range(1, g + 2):
                sh = 2 ** (s - 1)
                lo = 2 ** s
                nxt = scr[(s - 1) % 2]
                kb.op("dve", lambda e, cur=cur, nxt=nxt, lo=lo, sh=sh: e.tensor_tensor(
                    out=nxt[:, lo:hi], in0=cur[:, lo:hi], in1=cur[:, lo - sh:hi - sh], op=ALU.add),
                    reads=ckey, writes=[("scr", (s - 1) % 2)])
                cur = nxt
                ckey = [("scr", (s - 1) % 2)]
            kb.op("dve", lambda e, cur=cur, c=c, w=w: e.scalar_tensor_tensor(
                out=xTb[:, c, C0:hi], in0=cur[:, C0:hi], scalar=1.0 / w, in1=xT[:, c, C0:hi],
                op0=ALU.mult, op1=ALU.subtract),
                reads=ckey + [("xT", c)], writes=[("xTb", c)])
            if p == 0:
                kb.op("dve", lambda e, cur=cur, g=g: e.tensor_tensor(
                    out=small[:, :], in0=cur[:, C0:C0 + 16], in1=invc[:, g * 16:(g + 1) * 16], op=ALU.mult),
                    reads=ckey + ["cst"], writes=["small"])
                kb.op("dve", lambda e, c=c: e.tensor_tensor(
                    out=xTb[:, c, C0:C0 + 16], in0=small[:, :], in1=xT[:, c, C0:C0 + 16], op=ALU.subtract),
                    reads=["small", ("xT", c)], writes=[("xTb", c)])
        if p == 0:
            for g in range(4):
                w = 2 ** (g + 1)
                kb.op("dve", lambda e, g=g, w=w: e.tensor_reduce(
                    out=wsum[:, :, :], in_=stT[:, 4 * g:4 * g + 4, :, 16 - w:16], axis=AX.X, op=ALU.add),
                    reads=["stT"], writes=["wsum"])
                kb.op("dve", lambda e, g=g, w=w: e.scalar_tensor_tensor(
                    out=xTb[:, 4 * g:4 * g + 4, SC0:SC0 + NS], in0=wsum[:, :, :], scalar=1.0 / w,
                    in1=xT[:, 4 * g:4 * g + 4, SC0:SC0 + NS], op0=ALU.mult, op1=ALU.subtract),
                    reads=["wsum"] + [("xT", 4 * g + k) for k in range(4)], writes=[("xTb", 4 * g + k) for k in range(4)])
        emit_fill(cfg.get("pool_fill", 0))
        pb = [0]
        for g in range(4):
            sA = load_WA(pool_w.ap()[j, g].rearrange("(c p) n -> p c n", p=128), 4, 512)
            Wv = wa_view(sA, 4, 512)
            for m in range(4):
                for gi, (g0, gn) in enumerate(gs):
                    bank = 4 + 2 * gi + (pb[0] % 2)
                    for ci in range(4):
                        kb.op("pe", lambda e, Wv=Wv, ci=ci, m=m, g=g, g0=g0, gn=gn, bank=bank: e.matmul(
                            ps[bank][:, 0:gn], lhsT=Wv[:, ci, m * 128:(m + 1) * 128],
                            rhs=xTb[:, 4 * g + ci, g0:g0 + gn], start=(ci == 0), stop=(ci == 3)),
                            reads=[("WA", sA), ("xTb", 4 * g + ci)], writes=[("ps", bank)], inc=(ci == 3))
                    cch = 4 * g + m
                    scol = ROW_PS + j * 16 + cch
                    kb.op("dve", lambda e, cch=cch, g0=g0, gn=gn, bank=bank, scol=scol: e.scalar_tensor_tensor(
                        out=xT[:, cch, g0:g0 + gn], in0=ps[bank][:, 0:gn], scalar=cols[:, scol:scol + 1],
                        in1=xT[:, cch, g0:g0 + gn], op0=ALU.mult, op1=ALU.add),
                        reads=[("ps", bank), ("xT", cch), "cols"] + [("xTb", 4 * g + k) for k in range(4)],
                        writes=[("xT", cch)])
                pb[0] += 1

    def emit_gla(p, l):
        j = l // 2
        kb.barrier()
        ar.reset()
        gs = groups(p)
        ntt = NTT + (1 if p == 0 else 0)
        qTs = [ar.bf16([128, 2, COLS]) for _ in range(2)]
        kTs = [ar.bf16([128, 2, COLS]) for _ in range(2)]
        vbs = [ar.bf16([128, NTT + 1, 512]), stage[0].bitcast(BF16)[:, 0:(NTT + 1) * 512].rearrange("p (a b) -> p a b", a=NTT + 1)]
        sogs = [ar.bf16([128, NTT + 1, 512]), stage[1].bitcast(BF16)[:, 0:(NTT + 1) * 512].rearrange("p (a b) -> p a b", a=NTT + 1)]
        onT = ar.bf16([128, 4, COLS])
        gkl = ar.f32([128, COLS])
        Sb = ar.bf16([128, 2, 512])
        gkt_all = ar.f32([128, NTT, 256])
        ebTs = [ar.f32([128, 256]) for _ in range(2)]
        enbTs = [ar.f32([128, 256]) for _ in range(2)]
        qtls = [ar.bf16([128, 2, 128]) for _ in range(2)]
        ktls = [ar.bf16([128, 2, 128]) for _ in range(2)]
        ktoks = [ar.bf16([128, 256]) for _ in range(2)]
        ATbs = [ar.bf16([128, 128]) for _ in range(2)]
        gkt, ktok = gkt_all[:, 0, :], ktoks[0]
        tmp = ar.f32([128, 512])
        onb = ar.bf16([128, 512])
        ssq = ar.f32([128, 2])
        if p == 0:
            aT = ar.f32([128, 32])
            kTf = ar.f32([128, 2, NS])
            Qm = ar.bf16([128, 2, NS, NS])
            Km = [ar.bf16([128, 256]) for _ in range(2)]
            Snb1 = ar.bf16([128, 2, 512])
            Snb = [Snb1, Snb1]
        kb.op("dve", lambda e: e.memset(ssq[:, :], 0.0), reads=[],
              writes=["ssq", ("stage", 0), ("stage", 1), ("vb", 1), ("sog", 1)])
        kb.op("sp", lambda e: e.dma_start(out=wgu[:], in_=d_wgu.ap()[j]), writes=["wgu"], dma=misc_sem())
        kb.op("sp", lambda e: e.dma_start(out=gbias_bc[:], in_=d_gbias.ap()[j:j + 1, :].partition_broadcast(128)[:, 0, :]),
              writes=["gbias"], dma=misc_sem())
        kb.op("sp", lambda e: e.dma_start(out=normw_bc[:], in_=d_normw.ap()[j:j + 1, :].partition_broadcast(128)[:, 0, :]),
              writes=["normw"], dma=misc_sem())
        sA = load_WA(w_in.ap()[j, :, 6144:6160].rearrange("(c p) r -> p c r", p=128), DC, 16)
        Wv = wa_view(sA, DC, 16)
        for gi, (g0, gn) in enumerate(gs):
            for dc in range(DC):
                kb.op("pe", lambda e, Wv=Wv, dc=dc, g0=g0, gn=gn: e.matmul(
                    ps[2][0:16, 0:gn], lhsT=Wv[:, dc, 0:16], rhs=xTb[:, dc, g0:g0 + gn],
                    start=(dc == 0), stop=(dc == DC - 1)),
                    reads=[("WA", sA), ("xTb", dc)], writes=[("ps", 2)], inc=(dc == DC - 1))
            kb.op("act", lambda e, g0=g0, gn=gn: e.copy(out=gkl[0:16, g0:g0 + gn], in_=ps[2][0:16, 0:gn]),
                  reads=[("ps", 2)], writes=["gkl"])

        pjb = [0]

        def proj_steps(h):
            hb = h % 2
            qT, kT, vb, sog = qTs[hb], kTs[hb], vbs[hb], sogs[hb]
            steps = []
            st = {}
            for base, dst, scl, nm in ((h * 256, qT, 1.0 / 16, "q"), (1024 + h * 256, kT, 1.0, "k")):
                for kc in range(2):
                    for gi, (g0, gn) in enumerate(gs):
                        def step(base=base, dst=dst, scl=scl, nm=nm, kc=kc, gi=gi, g0=g0, gn=gn):
                            if kc == 0 and gi == 0:
                                st["sA"] = load_WA(w_in.ap()[j, :, base:base + 256].rearrange("(c p) f -> p c f", p=128), DC, 256)
                            sA = st["sA"]
                            Wv = wa_view(sA, DC, 256)
                            bank = pjb[0] % 2
                            pjb[0] += 1
                            for dc in range(DC):
                                kb.op("pe", lambda e, dc=dc: e.matmul(
                                    ps[bank][:, 0:gn], lhsT=Wv[:, dc, kc * 128:(kc + 1) * 128],
                                    rhs=xTb[:, dc, g0:g0 + gn], start=(dc == 0), stop=(dc == DC - 1)),
                                    reads=[("WA", sA), ("xTb", dc)], writes=[("ps", bank)], inc=(dc == DC - 1))
                            kb.op("act", lambda e: e.activation(
                                out=dst[:, kc, g0:g0 + gn], in_=ps[bank][:, 0:gn], func=AF.Identity, scale=scl),
                                reads=[("ps", bank)], writes=[("qk", hb)])
                            if nm == "k" and gi == len(gs) - 1 and p == 0 and False:
                                pass
                        steps.append(step)
            for which, base in (("v", 2048 + h * 512), ("og", 4096 + h * 512)):
                for half in range(2):
                    for tt in range(ntt):
                        def step(which=which, base=base, half=half, tt=tt):
                            if tt == 0:
                                st["sA"] = load_WA(w_in.ap()[j, :, base + half * 256:base + half * 256 + 256].rearrange(
                                    "(c p) f -> p c f", p=128), DC, 256)
                            sA = st["sA"]
                            Wv = wa_view(sA, DC, 256)
                            c0, M = (C0 + tt * 128, 128) if tt < NTT else (SC0, NS)
                            bank = pjb[0] % 2
                            pjb[0] += 1
                            for dc in range(DC):
                                kb.op("pe", lambda e, dc=dc: e.matmul(
                                    ps[bank][0:M, 0:256], lhsT=xTb[:, dc, c0:c0 + M], rhs=Wv[:, dc, 0:256],
                                    start=(dc == 0), stop=(dc == DC - 1)),
                                    reads=[("WA", sA), ("xTb", dc)], writes=[("ps", bank)], inc=(dc == DC - 1))
                            if which == "v":
                                kb.op("act", lambda e: e.copy(
                                    out=vb[0:M, tt, half * 256:(half + 1) * 256], in_=ps[bank][0:M, 0:256]),
                                    reads=[("ps", bank)], writes=[("vb", hb)])
                            else:
                                kb.op("act", lambda e: e.activation(
                                    out=sog[0:M, tt, half * 256:(half + 1) * 256], in_=ps[bank][0:M, 0:256], func=AF.Silu),
                                    reads=[("ps", bank)], writes=[("sog", hb)])
                        steps.append(step)
            return steps

        def chain_steps(h):
            hb = h % 2
            qT, kT, vb, sog = qTs[hb], kTs[hb], vbs[hb], sogs[hb]
            QK, VB, SOG = ("qk", hb), ("vb", hb), ("sog", hb)
            steps = []
            A = steps.append

            def o_epilogue_steps(M, obank, tt, c0):
                def s1():
                    kb.op("act", lambda e: e.activation(out=tmp[0:M, :], in_=ps[obank][0:M, :], func=AF.Square,
                                                        accum_out=ssq[0:M, 0:1]),
                          reads=[("ps", obank)], writes=["tmp", "ssq"])
                    kb.op("act", lambda e: e.activation(out=ssq[0:M, 0:1], in_=ssq[0:M, 0:1], func=AF.Sqrt,
                                                        bias=epsc[0:M, 1:2], scale=1.0 / 512),
                          reads=["ssq", "epsc"], writes=["ssq"])
                    kb.op("dve", lambda e: e.reciprocal(out=ssq[0:M, 0:1], in_=ssq[0:M, 0:1]), reads=["ssq"], writes=["ssq"])
                    kb.op("dve", lambda e: e.scalar_tensor_tensor(out=tmp[0:M, :], in0=ps[obank][0:M, :], scalar=ssq[0:M, 0:1],
                                                                  in1=normw_bc[0:M, :], op0=ALU.mult, op1=ALU.mult),
                          reads=[("ps", obank), "ssq", "normw"], writes=["tmp"])
                    kb.op("dve", lambda e: e.tensor_tensor(out=onb[0:M, :], in0=tmp[0:M, :], in1=sog[0:M, tt, :], op=ALU.mult),
                          reads=["tmp", SOG], writes=["onb"])
                A(s1)

                def s2():
                    for vc in range(4):
                        kb.op("pe", lambda e, vc=vc: e.transpose(out=psb[4][:, vc * M:(vc + 1) * M],
                                                                  in_=onb[0:M, vc * 128:(vc + 1) * 128],
                                                                  identity=identb[0:M, 0:M]),
                              reads=["onb", "identb"], writes=[("ps", 4)], inc=(vc == 3))
                    kb.op("act", lambda e: e.copy(out=onT[:, :, c0:c0 + M],
                                                  in_=psb[4][:, 0:4 * M].rearrange("p (a t) -> p a t", a=4)),
                          reads=[("ps", 4)], writes=[("onT", k) for k in range(4)])
                A(s2)

            def s_init():
                if p == 0:
                    kb.op("dve", lambda e: e.memset(Sst[:], 0.0), writes=["S"])
                else:
                    kb.op("sp", lambda e: e.dma_start(out=Sst[:], in_=ngp.ap()[j, h].rearrange("(c p) v -> p c v", p=128)),
                          reads=["ngp"], writes=["S"], dma=misc_sem())
                kb.op("act", lambda e: e.copy(out=Sb[:, :, :], in_=Sst[:]), reads=["S"], writes=["Sb"])
            A(s_init)
            def gate_all():
                nb = (NTT + 1) // 2
                for tt in range(NTT):
                    c0 = C0 + tt * 128
                    bank = 6 + tt // 2
                    col = (tt % 2) * 256
                    kb.op("pe", lambda e, c0=c0, bank=bank, col=col: e.matmul(
                        ps[bank][:, col:col + 256], lhsT=gkl[0:16, c0:c0 + 128],
                        rhs=wgu[0:16, h * 256:(h + 1) * 256], start=True, stop=True),
                        reads=["gkl", "wgu"], writes=[("ps", bank)], inc=(tt % 2 == 1 or tt == NTT - 1))
                for b in range(nb):
                    nbk = min(2, NTT - 2 * b)
                    kb.op("dve", lambda e, b=b, nbk=nbk: e.tensor_tensor(
                        out=gkt_all[:, 2 * b:2 * b + nbk, :],
                        in0=ps[6 + b][:, 0:nbk * 256].rearrange("p (a t) -> p a t", a=nbk),
                        in1=gbias_bc[:, h * 256:(h + 1) * 256].unsqueeze(1).to_broadcast([128, nbk, 256]), op=ALU.add),
                        reads=[("ps", 6 + b), "gbias"], writes=["gkt_all"])
                kb.op("act", lambda e: e.activation(out=gkt_all[:, :, :], in_=gkt_all[:, :, :], func=AF.Exp, scale=-1.0),
                      reads=["gkt_all"], writes=["gkt_all"])
                kb.op("act", lambda e: e.activation(out=gkt_all[:, :, :], in_=gkt_all[:, :, :], func=AF.Ln, bias=1.0, scale=1.0),
                      reads=["gkt_all"], writes=["gkt_all"])
            A(gate_all)
            fronts, backs = [], []
            for tt in range(NTT):
                c0 = C0 + tt * 128
                pr = tt % 2
                gk_, eb_, enb_, qtl, ktl, ktk, ATb = gkt_all[:, tt, :], ebTs[pr], enbTs[pr], qtls[pr], ktls[pr], ktoks[pr], ATbs[pr]
                KG, KE, KN, KQ, KK, KT, KA = "gkt_all", ("ebT", pr), ("enbT", pr), ("qtl", pr), ("ktl", pr), ("ktok", pr), ("ATb", pr)
                F, Bk = [], []

                def b2(c0=c0, gk_=gk_, eb_=eb_, enb_=enb_, qtl=qtl, ktl=ktl, KG=KG, KE=KE, KN=KN, KQ=KQ, KK=KK):
                    for kc in range(2):
                        kb.op("pe", lambda e, kc=kc: e.matmul(ps[3][:, kc * 128:(kc + 1) * 128],
                                                              lhsT=gk_[:, kc * 128:(kc + 1) * 128], rhs=Umask,
                                                              start=True, stop=True),
                              reads=[KG, "cst"], writes=[("ps", 3)], inc=(kc == 1))
                    kb.op("act", lambda e: e.activation(out=eb_[:, :], in_=ps[3][:, 0:256], func=AF.Exp, scale=-1.0 / 16),
                          reads=[("ps", 3)], writes=[KE])
                    kb.op("act", lambda e: e.activation(out=enb_[:, :], in_=ps[3][:, 0:256], func=AF.Exp, scale=1.0 / 16),
                          reads=[("ps", 3), KE], writes=[KN])
                    kb.op("dve", lambda e: e.tensor_tensor(out=qtl[:, :, :], in0=qT[:, :, c0:c0 + 128],
                                                           in1=eb_.rearrange("p (a t) -> p a t", a=2), op=ALU.mult),
                          reads=[QK, KE], writes=[KQ])
                    kb.op("dve", lambda e: e.tensor_tensor(out=ktl[:, :, :], in0=kT[:, :, c0:c0 + 128],
                                                           in1=enb_.rearrange("p (a t) -> p a t", a=2), op=ALU.mult),
                          reads=[QK, KN], writes=[KK])
                F.append(b2)

                def b3(qtl=qtl, ktl=ktl, ktk=ktk, ATb=ATb, KQ=KQ, KK=KK, KT=KT, KA=KA):
                    for kc in range(2):
                        kb.op("pe", lambda e, kc=kc: e.transpose(out=psb[3][:, 512 + kc * 128:512 + (kc + 1) * 128], in_=ktl[:, kc, :],
                                                                  identity=identb[:]),
                              reads=[KK, "identb"], writes=[("ps", 3)], inc=(kc == 1))
                    kb.op("act", lambda e: e.copy(out=ktk[:, :], in_=psb[3][:, 512:768]), reads=[("ps", 3)], writes=[KT])
                    for kc in range(2):
                        kb.op("pe", lambda e, kc=kc: e.matmul(ps[2][:, 256:384], lhsT=ktl[:, kc, :], rhs=qtl[:, kc, :],
                                                              start=(kc == 0), stop=(kc == 1)),
                              reads=[KK, KQ], writes=[("ps", 2)], inc=(kc == 1))
                    kb.op("dve", lambda e: e.tensor_tensor(out=ATb[:, :], in0=ps[2][:, 256:384], in1=Umask, op=ALU.mult),
                          reads=[("ps", 2), "cst"], writes=[KA])
                F.append(b3)

                def b4(tt=tt, eb_=eb_, qtl=qtl, ktk=ktk, ATb=ATb, KE=KE, KQ=KQ, KT=KT, KA=KA):
                    kb.op("pe", lambda e: e.matmul(ps[5][:, :], lhsT=ATb[:, :], rhs=vb[:, tt, :], start=True, stop=False),
                          reads=[KA, VB], writes=[("ps", 5)], inc=False)
                    for kc in range(2):
                        kb.op("pe", lambda e, kc=kc: e.matmul(ps[5][:, :], lhsT=qtl[:, kc, :], rhs=Sb[:, kc, :],
                                                              start=False, stop=(kc == 1)),
                              reads=[KQ, "Sb"], writes=[("ps", 5)], inc=(kc == 1))
                    for kc in range(2):
                        kb.op("pe", lambda e, kc=kc: e.matmul(ps[6 + kc][:, :], lhsT=ktk[:, kc * 128:(kc + 1) * 128],
                                                              rhs=vb[:, tt, :], start=True, stop=True),
                              reads=[KT, VB], writes=[("ps", 6 + kc)])
                        kb.op("dve", lambda e, kc=kc: e.tensor_tensor(out=Sst[:, kc, :], in0=ps[6 + kc][:, :],
                                                                      in1=Sst[:, kc, :], op=ALU.add),
                              reads=[("ps", 6 + kc), "S"], writes=["S"])
                        kb.op("dve", lambda e, kc=kc: e.tensor_scalar(out=Sst[:, kc, :], in0=Sst[:, kc, :],
                                                                      scalar1=eb_[:, kc * 128 + 127:kc * 128 + 128],
                                                                      scalar2=None, op0=ALU.mult),
                              reads=["S", KE], writes=["S"])
                    kb.op("act", lambda e: e.copy(out=Sb[:, :, :], in_=Sst[:]), reads=["S"], writes=["Sb"])
                Bk.append(b4)
                n0 = len(steps)
                o_epilogue_steps(128, 5, tt, c0)
                Bk.extend(steps[n0:])
                del steps[n0:]
                fronts.append(F)
                backs.append(Bk)
            steps.extend(fronts[0])
            for tt in range(NTT):
                nxt = fronts[tt + 1] if tt + 1 < NTT else []
                bk = backs[tt]
                for i in range(max(len(bk), len(nxt))):
                    if i < len(bk):
                        steps.append(bk[i])
                    if i < len(nxt):
                        steps.append(nxt[i])

            def s_fin():
                kb.op("sp", lambda e: e.dma_start(out=ngp.ap()[j, h].rearrange("(c p) v -> p c v", p=128), in_=Sst[:]),
                      reads=["S"], writes=["ngp"], dma=misc_sem())
            A(s_fin)
            if p == 0:
                tt = NTT

                def c1():
                    kb.op("pe", lambda e: e.matmul(ps[2][0:NS, 0:256], lhsT=gkl[0:16, SC0:SC0 + NS],
                                                   rhs=wgu[0:16, h * 256:(h + 1) * 256], start=True, stop=True),
                          reads=["gkl", "wgu"], writes=[("ps", 2)])
                    kb.op("dve", lambda e: e.tensor_tensor(out=gkt[0:NS, :], in0=ps[2][0:NS, 0:256],
                                                           in1=gbias_bc[0:NS, h * 256:(h + 1) * 256], op=ALU.add),
                          reads=[("ps", 2), "gbias"], writes=["gkt_all"])
                    kb.op("act", lambda e: e.activation(out=gkt[0:NS, :], in_=gkt[0:NS, :], func=AF.Exp, scale=-1.0),
                          reads=["gkt_all"], writes=["gkt_all"])
                    kb.op("act", lambda e: e.activation(out=gkt[0:NS, :], in_=gkt[0:NS, :], func=AF.Ln, bias=1.0, scale=1.0),
                          reads=["gkt_all"], writes=["gkt_all"])
                A(c1)

                def c2():
                    for kc in range(2):
                        kb.op("pe", lambda e, kc=kc: e.matmul(ps[3][:, kc * NS:(kc + 1) * NS],
                                                              lhsT=gkt[0:NS, kc * 128:(kc + 1) * 128], rhs=cst[0:NS, 0:NS],
                                                              start=True, stop=True),
                              reads=["gkt_all", "cst"], writes=[("ps", 3)], inc=(kc == 1))
                    kb.op("act", lambda e: e.activation(out=aT[:, :], in_=ps[3][:, 0:2 * NS], func=AF.Exp, scale=-1.0 / 16),
                          reads=[("ps", 3)], writes=["aT"])
                    kb.op("dve", lambda e: e.tensor_copy(out=kTf[:, :, :], in_=kT[:, :, SC0:SC0 + NS]), reads=[QK], writes=["kTf"])
                A(c2)

                def c3():
                    for kc in range(2):
                        kb.op("pe", lambda e, kc=kc: e.transpose(out=ps[3][0:NS, 256 + kc * 128:256 + (kc + 1) * 128],
                                                                  in_=kTf[:, kc, :], identity=ident),
                              reads=["kTf", "cst"], writes=[("ps", 3)], inc=(kc == 1))
                    kb.op("act", lambda e: e.copy(out=ktok[0:NS, :], in_=ps[3][0:NS, 256:512]), reads=[("ps", 3)], writes=[("ktok", 0)])
                    for kc in range(2):
                        kb.op("dve", lambda e, kc=kc: e.tensor_tensor(
                            out=Qm[:, kc, :, :], in0=qT[:, kc, SC0:SC0 + NS].unsqueeze(2).to_broadcast([128, NS, NS]),
                            in1=eyeb, op=ALU.mult),
                            reads=[QK, "cst"], writes=["Qm"])
                A(c3)
                for s_ in range(NS):
                    def d1(s=s_):
                        b2_ = s % 2
                        for sl in ([0, 1] if s == 0 else [s + 1]):
                            if sl < NS:
                                kb.op("sp", lambda e, sl=sl: e.dma_start(
                                    out=Ss[sl % 2][:], in_=sgla.ap()[j, sl, h].rearrange("(c p) v -> p c v", p=128)),
                                    writes=[("Ss", sl % 2)], dma="ss%d" % (sl % 2))
                        dsb = (6, 7) if b2_ == 0 else (5, 2)
                        kb.op("dve", lambda e: e.tensor_scalar(out=Km[b2_][0:NS, :], in0=ktok[0:NS, :],
                                                               scalar1=cst[0:NS, s:s + 1], scalar2=None, op0=ALU.mult),
                              reads=[("ktok", 0), "cst"], writes=[("Km", b2_)])
                        for kc in range(2):
                            kb.op("pe", lambda e, kc=kc: e.matmul(ps[dsb[kc]][:, :], lhsT=Km[b2_][0:NS, kc * 128:(kc + 1) * 128],
                                                                  rhs=vb[0:NS, tt, :], start=True, stop=True),
                                  reads=[("Km", b2_), VB], writes=[("ps", dsb[kc])])
                            kb.op("dve", lambda e, kc=kc: e.scalar_tensor_tensor(
                                out=Sn[b2_][:, kc, :], in0=Ss[b2_][:, kc, :], scalar=aT[:, kc * NS + s:kc * NS + s + 1],
                                in1=ps[dsb[kc]][:, :], op0=ALU.mult, op1=ALU.add),
                                reads=[("Ss", b2_), "aT", ("ps", dsb[kc])], writes=[("Sn", b2_)])
                        kb.op("sp", lambda e: e.dma_start(
                            out=ngs.ap()[j, s, h].rearrange("(c p) v -> p c v", p=128), in_=Sn[b2_][:]),
                            reads=[("Sn", b2_)], dma="sn%d" % b2_)
                        kb.op("act", lambda e: e.copy(out=Snb[b2_][:, :, :], in_=Sn[b2_][:]),
                              reads=[("Sn", b2_)], writes=["Snb"])
                    A(d1)

                    def d2(s=s_):
                        b2_ = s % 2
                        for kc in range(2):
                            kb.op("pe", lambda e, kc=kc: e.matmul(
                                ps[4][0:NS, :], lhsT=Qm[:, kc, s, :], rhs=Snb[b2_][:, kc, :],
                                start=(s == 0 and kc == 0), stop=(s == NS - 1 and kc == 1)),
                                reads=["Qm", "Snb"], writes=[("ps", 4)], inc=(kc == 1))
                    A(d2)
                o_epilogue_steps(NS, 4, tt, SC0)
            return steps

        def wout(h):
            sB = load_WB(w_out.ap()[j, h * 512:(h + 1) * 512, :].rearrange("(c p) d -> p c d", p=128))
            emit_accum_mm(gs, sB, onT, "onT", 4)

        for st_ in proj_steps(0):
            st_()
        for h in range(4):
            cs = chain_steps(h)
            psn = proj_steps(h + 1) if h < 3 else []
            ci = 0
            for pi, pstep in enumerate(psn):
                pstep()
                want = (len(cs) * (pi + 1)) // max(len(psn), 1)
                while ci < want:
                    cs[ci]()
                    ci += 1
            while ci < len(cs):
                cs[ci]()
                ci += 1
            wout(h)
        kb.op("dve", lambda e: e.memset(ssq[:, :], 0.0), reads=[],
              writes=["ssq", ("stage", 0), ("stage", 1), ("vb", 1), ("sog", 1)])

    for p in range(NPASS):
        emit_in(p)
        for l in range(NL):
            if l % 2 == 0:
                if MIX in ("pool", "all"):
                    emit_pool(p, l)
            else:
                if MIX == "all":
                    emit_gla(p, l)
            if cfg.get("do_ln", True):
                emit_ln(p, ROW_GM + l * 16, ROW_BM + l * 16, barrier=not (l % 2 == 0 and p > 0 and MIX != "none"))
            if cfg.get("do_mlp", True):
                emit_mlp(p, l)
            if cfg.get("do_ln", True):
                emit_ln(p, ROW_GF + l * 16, ROW_BF + l * 16, barrier=False)
        emit_out(p)

    kb.emit()
    return nc


def make_consts():
    c = np.zeros((128, CW), np.float32)
    c[:, 0:128] = np.eye(128, dtype=np.float32)
    j = np.arange(128)[:, None]
    i = np.arange(128)[None, :]
    c[:, 128:256] = (j <= i).astype(np.float32)
    for g in range(4):
        w = 2 ** (g + 1)
        t = np.arange(16)
        c[:, 256 + g * 16:256 + (g + 1) * 16] = (1.0 / np.minimum(t + 1, w))[None, :]
    c[:, 320:576] = np.eye(16, dtype=np.float32).reshape(1, 256)
    return c


def make_vecs(inp):
    rows = [
        np.asarray(inp["mlp_b1"], np.float32).reshape(-1, 128),
        np.asarray(inp["mlp_b2"], np.float32).reshape(-1, 128),
        np.asarray(inp["ln_mix_g"], np.float32).reshape(-1, 128),
        np.asarray(inp["ln_mix_b"], np.float32).reshape(-1, 128),
        np.asarray(inp["ln_ffn_g"], np.float32).reshape(-1, 128),
        np.asarray(inp["ln_ffn_b"], np.float32).reshape(-1, 128),
        np.asarray(inp["pool_scale"], np.float32).reshape(-1, 128),
    ]
    v = np.concatenate(rows, axis=0)
    out = np.zeros((NROWS_PAD, 128), np.float32)
    out[: v.shape[0]] = v
    return out


N_CORES = 8
PROMPT_CORES = (0, 1, 4, 5)
TP_RUN = 512
_NC_CACHE = {}


def kernel(x_prompt, x_sample, state_pool, state_gla, pool_w, pool_scale, gla_w_in, gla_w_gate_up,
           gla_gate_bias, gla_norm_w, gla_w_out, ln_mix_g, ln_mix_b, mlp_w1, mlp_b1, mlp_w2, mlp_b2,
           ln_ffn_g, ln_ffn_b):
    f = lambda a: np.ascontiguousarray(np.asarray(a), dtype=np.float32)
    x_prompt = f(x_prompt)
    x_sample = f(x_sample)
    state_pool = f(state_pool)
    state_gla = f(state_gla)
    B, L, _ = x_prompt.shape
    npass = L // TP_RUN
    key = (npass, TP_RUN)
    if key not in _NC_CACHE:
        _NC_CACHE[key] = build(dict(npass=npass, tp=TP_RUN, layers=DEPTH, mixers="all"))
    nc = _NC_CACHE[key]
    vec_in = dict(mlp_b1=mlp_b1, mlp_b2=mlp_b2, ln_mix_g=ln_mix_g, ln_mix_b=ln_mix_b, ln_ffn_g=ln_ffn_g,
                  ln_ffn_b=ln_ffn_b, pool_scale=pool_scale)
    shared = dict(vecs=make_vecs(vec_in), consts=make_consts(), mlp_w1=f(mlp_w1), mlp_w2=f(mlp_w2),
                  pool_w=f(pool_w), gla_w_in=f(gla_w_in), gla_wgu=f(gla_w_gate_up), gla_gbias=f(gla_gate_bias),
                  gla_normw=f(gla_norm_w), gla_w_out=f(gla_w_out))
    in_maps = []
    real = {PROMPT_CORES[b]: b for b in range(B)}
    zero_prompt = np.zeros_like(x_prompt[0])
    for c in range(N_CORES):
        m = dict(shared)
        m["xp"] = x_prompt[real[c]] if c in real else zero_prompt
        m["xs"] = x_sample[c * NS:(c + 1) * NS, 0]
        m["spool"] = np.ascontiguousarray(state_pool[:, c * NS:(c + 1) * NS])
        m["sgla"] = np.ascontiguousarray(state_gla[:, c * NS:(c + 1) * NS])
        in_maps.append(m)
    res = run_bass_kernel_spmd(nc, in_maps, core_ids=list(range(N_CORES)))
    r = res.results
    pc = PROMPT_CORES
    y_prompt = np.stack([r[pc[b]]["yp"] for b in range(B)], axis=0)
    y_sample = np.concatenate([r[c]["ys"] for c in range(N_CORES)], axis=0)[:, None, :]
    new_pool_prompt = np.stack([r[pc[b]]["npp"] for b in range(B)], axis=1)
    new_gla_prompt = np.stack([r[pc[b]]["ngp"] for b in range(B)], axis=1)
    new_pool_sample = np.concatenate([r[c]["nps"] for c in range(N_CORES)], axis=1)
    new_gla_sample = np.concatenate([r[c]["ngs"] for c in range(N_CORES)], axis=1)
    return (y_prompt.astype(np.float32), y_sample.astype(np.float32), new_pool_prompt.astype(np.float32),
            new_gla_prompt.astype(np.float32), new_pool_sample.astype(np.float32), new_gla_sample.astype(np.float32))
```
